# Optimizing a Trainium2 kernel written in Bass

```python
import jax, jax.numpy as jnp
from jax import lax
import numpy as np

D_MODEL = 2048
BATCH = 8
SEQ = 4096
DEPTH = 2

CHUNK = 64
Q_BLOCK = 128
CONV_DIM = D_MODEL
CONV_WIDTH = 31
N_HEADS = 16
QK_NOPE = 128
QK_ROPE = 64
QK_HEAD = QK_NOPE + QK_ROPE
V_HEAD = 128
MLA_DIM = N_HEADS * V_HEAD
Q_LORA = 512
KV_LORA = 512
ROPE_THETA = 10000.0
EPS = 1e-6
IN_SIZES = (CONV_DIM, CONV_DIM, CONV_DIM, Q_LORA, KV_LORA, QK_ROPE, MLA_DIM, D_MODEL, D_MODEL)
IN_DIM = 3 * CONV_DIM + Q_LORA + KV_LORA + QK_ROPE + MLA_DIM + 2 * D_MODEL

kernel_name = "hybrid_conformer_mla_gated_merge"


def rmsnorm(x, g):
    xf = x.astype(jnp.float32)
    y = xf * lax.rsqrt(jnp.mean(xf * xf, axis=-1, keepdims=True) + EPS)
    return (y * g.astype(jnp.float32)).astype(x.dtype)


def layernorm(x, g, b):
    xf = x.astype(jnp.float32)
    mu = jnp.mean(xf, axis=-1, keepdims=True)
    xc = xf - mu
    y = xc * lax.rsqrt(jnp.mean(xc * xc, axis=-1, keepdims=True) + EPS)
    return (y * g.astype(jnp.float32) + b.astype(jnp.float32)).astype(x.dtype)


def rope(x, cos, sin):
    half = x.shape[-1] // 2
    x1, x2 = x[..., :half], x[..., half:]
    return jnp.concatenate([x1 * cos - x2 * sin, x1 * sin + x2 * cos], axis=-1)


def rope_tables(positions, dtype):
    inv_freq = ROPE_THETA ** (-jnp.arange(0, QK_ROPE, 2, dtype=jnp.float32) / QK_ROPE)
    ang = positions.astype(jnp.float32)[..., None] * inv_freq
    cos = jnp.cos(ang)[:, :, None, :].astype(dtype)
    sin = jnp.sin(ang)[:, :, None, :].astype(dtype)
    return cos, sin


def chunk_causal_attention(q, k, v):
    B, S, H, Dh = q.shape
    nb = S // Q_BLOCK
    scale = QK_HEAD ** -0.5
    key_chunk = jnp.arange(S) // CHUNK
    qb = q.reshape(B, nb, Q_BLOCK, H, Dh).transpose(1, 0, 2, 3, 4)

    def block(args):
        q_i, i = args
        s = jnp.einsum('bqhd,bkhd->bhqk', q_i, k,
                       preferred_element_type=jnp.float32) * scale
        q_chunk = (i * Q_BLOCK + jnp.arange(Q_BLOCK)) // CHUNK
        mask = key_chunk[None, :] <= q_chunk[:, None]
        s = jnp.where(mask, s, -jnp.inf)
        p = jax.nn.softmax(s, axis=-1).astype(v.dtype)
        return jnp.einsum('bhqk,bkhd->bqhd', p, v)

    o = lax.map(block, (qb, jnp.arange(nb)))
    return o.transpose(1, 0, 2, 3, 4).reshape(B, S, H, v.shape[-1])


def conv_branch(a, glu, gate, conv_w, conv_b, ln_g, ln_b):
    u = a * jax.nn.sigmoid(glu)
    u = lax.conv_general_dilated(
        u, conv_w[:, None, :].astype(u.dtype), window_strides=(1,),
        padding=[(CONV_WIDTH - 1, 0)],
        dimension_numbers=('NWC', 'WIO', 'NWC'),
        feature_group_count=CONV_DIM) + conv_b
    u = jax.nn.silu(layernorm(u, ln_g, ln_b))
    return u * jax.nn.silu(gate)


def mla_branch(q_down, kv_down, k_pe, gate, cos, sin,
               q_a_g, w_uq, kv_a_g, w_ukv, q_norm_g, k_norm_g):
    B, S, _ = q_down.shape
    q = (rmsnorm(q_down, q_a_g) @ w_uq).reshape(B, S, N_HEADS, QK_HEAD)
    kv = (rmsnorm(kv_down, kv_a_g) @ w_ukv).reshape(B, S, N_HEADS, QK_NOPE + V_HEAD)
    k_nope, v = kv[..., :QK_NOPE], kv[..., QK_NOPE:]
    k_pe_h = jnp.broadcast_to(k_pe[:, :, None, :], (B, S, N_HEADS, QK_ROPE))
    k = jnp.concatenate([k_nope, k_pe_h], axis=-1)
    q = rmsnorm(q, q_norm_g)
    k = rmsnorm(k, k_norm_g)
    q = jnp.concatenate([q[..., :QK_NOPE], rope(q[..., QK_NOPE:], cos, sin)], axis=-1)
    k = jnp.concatenate([k[..., :QK_NOPE], rope(k[..., QK_NOPE:], cos, sin)], axis=-1)
    o = chunk_causal_attention(q, k, v)
    return o.reshape(B, S, MLA_DIM) * jax.nn.silu(gate)


def hybrid_layer(x, cos, sin, norm_g, w_in, conv_w, conv_b, conv_ln_g, conv_ln_b,
                 q_a_g, w_uq, kv_a_g, w_ukv, q_norm_g, k_norm_g,
                 w_proj_conv, w_proj_mla, w_out):
    h = rmsnorm(x, norm_g)
    proj = h @ w_in
    idx = [int(i) for i in np.cumsum(IN_SIZES)[:-1]]
    (c_a, c_glu, c_gate, q_down, kv_down, k_pe, m_gate,
     g_conv, g_mla) = jnp.split(proj, idx, axis=-1)
    y_conv = conv_branch(c_a, c_glu, c_gate, conv_w, conv_b, conv_ln_g, conv_ln_b) @ w_proj_conv
    y_mla = mla_branch(q_down, kv_down, k_pe, m_gate, cos, sin,
                       q_a_g, w_uq, kv_a_g, w_ukv, q_norm_g, k_norm_g) @ w_proj_mla
    y = jax.nn.sigmoid(g_conv) * y_conv + jax.nn.sigmoid(g_mla) * y_mla
    return x + y @ w_out


def setup_inputs(seed: int = 0) -> dict:
    key = jax.random.key(seed)
    ks = jax.random.split(key, 24)
    f32 = jnp.float32

    def nrm(k, shape, scale):
        return jax.random.normal(k, shape, f32) * scale

    def gain(k, shape):
        return 1.0 + 0.02 * jax.random.normal(k, shape, f32)

    x = jax.random.normal(ks[0], (BATCH, SEQ, D_MODEL), f32)
    offsets = jax.random.randint(ks[1], (BATCH, 1), 0, 1024, dtype=jnp.int32)
    positions = offsets + jnp.arange(SEQ, dtype=jnp.int32)[None, :]
    return {
        "x": x,
        "positions": positions,
        "norm_g": gain(ks[2], (DEPTH, D_MODEL)),
        "w_in": nrm(ks[3], (DEPTH, D_MODEL, IN_DIM), D_MODEL ** -0.5),
        "conv_w": nrm(ks[4], (DEPTH, CONV_WIDTH, CONV_DIM), CONV_WIDTH ** -0.5),
        "conv_b": nrm(ks[5], (DEPTH, CONV_DIM), 0.02),
        "conv_ln_g": gain(ks[6], (DEPTH, CONV_DIM)),
        "conv_ln_b": nrm(ks[7], (DEPTH, CONV_DIM), 0.02),
        "q_a_g": gain(ks[8], (DEPTH, Q_LORA)),
        "w_uq": nrm(ks[9], (DEPTH, Q_LORA, N_HEADS * QK_HEAD), Q_LORA ** -0.5),
        "kv_a_g": gain(ks[10], (DEPTH, KV_LORA)),
        "w_ukv": nrm(ks[11], (DEPTH, KV_LORA, N_HEADS * (QK_NOPE + V_HEAD)), KV_LORA ** -0.5),
        "q_norm_g": gain(ks[12], (DEPTH, QK_HEAD)),
        "k_norm_g": gain(ks[13], (DEPTH, QK_HEAD)),
        "w_proj_conv": nrm(ks[14], (DEPTH, CONV_DIM, D_MODEL), CONV_DIM ** -0.5),
        "w_proj_mla": nrm(ks[15], (DEPTH, MLA_DIM, D_MODEL), MLA_DIM ** -0.5),
        "w_out": nrm(ks[16], (DEPTH, D_MODEL, D_MODEL), D_MODEL ** -0.5),
    }


def reference(x, positions, norm_g, w_in, conv_w, conv_b, conv_ln_g, conv_ln_b,
              q_a_g, w_uq, kv_a_g, w_ukv, q_norm_g, k_norm_g,
              w_proj_conv, w_proj_mla, w_out):
    cos, sin = rope_tables(positions, x.dtype)
    for l in range(DEPTH):
        x = hybrid_layer(x, cos, sin, norm_g[l], w_in[l], conv_w[l], conv_b[l],
                         conv_ln_g[l], conv_ln_b[l], q_a_g[l], w_uq[l], kv_a_g[l], w_ukv[l],
                         q_norm_g[l], k_norm_g[l], w_proj_conv[l], w_proj_mla[l], w_out[l])
    return x
```

```python
from contextlib import ExitStack
import numpy as np
import concourse.bass as bass
import concourse.mybir as mybir
from concourse.bass_utils import run_bass_kernel_spmd

F32 = mybir.dt.float32
BF16 = mybir.dt.bfloat16
I32 = mybir.dt.int32
AF = mybir.ActivationFunctionType
ALU = mybir.AluOpType

D = 2048
S = 4096
DEPTH = 2
NCH = 16
TT = 512
NTILE = S // TT
CW = 31
NH = 16
IN_DIM = 13376
EPS = 1e-6
OFF_A, OFF_GLU, OFF_CG, OFF_QD, OFF_KVD, OFF_KPE, OFF_MG, OFF_GC, OFF_GM = (
    0, 2048, 4096, 6144, 6656, 7168, 7232, 9280, 11328)
PI = float(np.pi)
A_PARTS = {"A1", "A2", "A3"}
A3_STOP = 9
WARM_N = 16
DVE_TAPS = 8
DEN_MODE = "pe"


class Buf:
    __slots__ = ("t", "w", "r", "ds", "acc", "excl")

    def __init__(self, t, acc=False, excl=False):
        self.excl = excl
        self.t = t
        self.w = {}
        self.r = {}
        self.ds = None
        self.acc = acc

    def __getitem__(self, idx):
        return self.t[idx]


class DSem:
    __slots__ = ("h", "tot", "key")

    def __init__(self, h, key):
        self.h = h
        self.tot = 0
        self.key = key


class Sched:
    def __init__(self, nc, stack, n_dsem=56):
        self.nc = nc
        self.eng = {"pe": nc.tensor, "act": nc.scalar, "dve": nc.vector, "pool": nc.gpsimd, "sp": nc.sync}
        self.sem = {k: stack.enter_context(nc.semaphore("cs_" + k)) for k in ("pe", "act", "dve", "pool")}
        self.cnt = {k: 0 for k in self.sem}
        self.seen = {k: {} for k in self.eng}
        self.all_dsems = [DSem(stack.enter_context(nc.semaphore("ds%d" % i)), "ds%d" % i) for i in range(n_dsem)]
        self.free_dsems = {"pool": list(self.all_dsems[:26]), "sp": list(self.all_dsems[26:])}
        self.ds_kind = {}
        for k_, lst in self.free_dsems.items():
            for d_ in lst:
                self.ds_kind[d_.key] = k_
        self.phase_bufs = []

    def _gather(self, reads, writes):
        deps = {}

        def add(d):
            for k, sv in d.items():
                if k not in deps or deps[k][1] < sv[1]:
                    deps[k] = sv
        for b in reads:
            add(b.w)
            if b.excl:
                add(b.r)
        for b in writes:
            if not b.acc:
                add(b.w)
            add(b.r)
        return deps

    def _wait(self, e, deps):
        seen = self.seen[e]
        for k, (sem, val) in deps.items():
            if seen.get(k, 0) < val:
                self.eng[e].wait_ge(sem, val)
                seen[k] = val

    def _record(self, key, d, reads, writes):
        for b in writes:
            if b.acc:
                b.w[key] = d
            else:
                b.w = {key: d}
                b.r = {}
        for b in reads:
            b.r[key] = d

    def op(self, e, fn, reads=(), writes=(), inc=True):
        deps = self._gather(reads, writes)
        if e == "pe":
            deps.pop("pe", None)
        self._wait(e, deps)
        ins = fn(self.eng[e])
        if inc:
            self.cnt[e] += 1
            ins.then_inc(self.sem[e], 1)
            self._record(e, (self.sem[e], self.cnt[e]), reads, writes)
        return ins

    def dsem_for(self, owner, q):
        if owner.ds is None:
            owner.ds = self.free_dsems[q].pop()
            self.phase_bufs.append(owner)
        assert self.ds_kind[owner.ds.key] == q, "buffer used with both DMA queue kinds"
        return owner.ds

    def dma(self, q, out, in_, owner, reads=(), writes=()):
        deps = self._gather(reads, writes)
        self._wait(q, deps)
        ds = self.dsem_for(owner, q)
        ins = self.eng[q].dma_start(out=out, in_=in_)
        ds.tot += 16
        ins.then_inc(ds.h, 16)
        self._record(ds.key, (ds.h, ds.tot), reads, writes)
        return ins

    def barrier(self, engines=("pe", "act", "dve", "pool", "sp"), skip_keep=False):
        deps = {k: (self.sem[k], self.cnt[k]) for k in self.sem if self.cnt[k] > 0}
        keep_keys = {b.ds.key for b in self.phase_bufs if getattr(b, "_keep", False) and b.ds is not None}
        for ds in self.all_dsems:
            if ds.tot > 0 and not (skip_keep and ds.key in keep_keys):
                deps[ds.key] = (ds.h, ds.tot)
        for e in engines:
            d = dict(deps)
            if e in d:
                pass
            self._wait(e, d)
        for b in self.phase_bufs:
            if not getattr(b, "_keep", False):
                self.free_dsems[self.ds_kind[b.ds.key]].append(b.ds)
                b.ds = None
        self.phase_bufs = [b for b in self.phase_bufs if b.ds is not None]


def build_program(depth=DEPTH, debug=False, phases="ABC", ntile_limit=None):
    nc = bass.Bass("TRN2", target_bir_lowering=False)
    dbg_kind = "ExternalOutput" if debug else "Internal"

    def din(name, shape, dt=F32):
        return nc.dram_tensor(name, list(shape), dt, kind="ExternalInput").ap()

    def dscr(name, shape, dt, dbg=True):
        return nc.dram_tensor(name, list(shape), dt, kind=(dbg_kind if dbg else "Internal")).ap()

    xT_in = din("xT", [D, S])
    pos_in = din("pos", [1, S], I32)
    w_in = din("w_in", [depth, D, IN_DIM])
    w_uq = din("w_uq", [depth, 512, 3072])
    w_ukv = din("w_ukv", [depth, 512, 4096])
    w_pc = din("w_proj_conv", [depth, D, D])
    w_pm = din("w_proj_mla", [depth, D, D])
    w_o = din("w_out", [depth, D, D])
    pv_in = din("pvec", [128, 5, depth, NCH])
    cw_in = din("convwT", [128, depth, NCH, CW])
    lg_in = din("lorag", [128, 2, depth, 4])
    hn_in = din("hnorm_n", [128, 2, depth])
    hr_in = din("hnorm_r", [64, 2, depth])
    cst_in = din("consts", [128, 128 + 64 + 1])
    y_out = nc.dram_tensor("yT", [D, S], F32, kind="ExternalOutput").ap()

    wb_in = [dscr("wb_in%d" % l, [D, IN_DIM], BF16, False) for l in range(depth)]
    wb_uq = [dscr("wb_uq%d" % l, [512, 3072], BF16, False) for l in range(depth)]
    wb_uk = [dscr("wb_uk%d" % l, [512, NH, 128], BF16, False) for l in range(depth)]
    wb_uv = [dscr("wb_uv%d" % l, [512, NH, 128], BF16, False) for l in range(depth)]
    wb_pc = [dscr("wb_pc%d" % l, [D, D], BF16, False) for l in range(depth)]
    wb_pm = [dscr("wb_pm%d" % l, [D, D], BF16, False) for l in range(depth)]
    wb_o = [dscr("wb_o%d" % l, [D, D], BF16, False) for l in range(depth)]
    cosT = dscr("cosT", [64, S], F32)
    sinT = dscr("sinT", [64, S], F32)
    cbrT = dscr("cbrT", [D, S], BF16)
    smgT = dscr("smgT", [D, S], BF16)
    tgcT = dscr("tgcT", [D, S], BF16)
    tgmT = dscr("tgmT", [D, S], BF16)
    qTn = dscr("qTn", [NH, 128, S], BF16)
    qTr = dscr("qTr", [NH, 64, S], BF16)
    kTn = dscr("kTn", [NH, 128, S], BF16)
    kTr = dscr("kTr", [64, S], BF16)
    Vd = dscr("Vd", [S, D], BF16)
    Fkd = dscr("Fkd", [S, NH], F32)
    attT = dscr("attT", [D, S], BF16)
    x1T = dscr("x1T", [D, S], F32)

    with ExitStack() as stack:
        sch = Sched(nc, stack)

        uid = {"i": 0}

        def uname(name):
            uid["i"] += 1
            return "s%d_%s" % (uid["i"], name)

        def sb(st, name, shape, dt, keep=False):
            b = Buf(st.enter_context(nc.sbuf_tensor(uname(name), list(shape), dt)))
            if keep:
                b._keep = True
            return b

        class KBuf(Buf):
            __slots__ = ("_keep",)

        def sbk(st, name, shape, dt):
            b = KBuf(st.enter_context(nc.sbuf_tensor(uname(name), list(shape), dt)))
            b._keep = False
            return b

        def dbuf():
            b = KBuf(None, acc=True)
            b._keep = False
            return b

        dr = {n: dbuf() for n in ("cos", "cbr", "smg", "tgc", "tgm", "qn", "qr", "kn", "kr", "V", "Fk", "att", "x1", "y")}
        dw = {}
        for l in range(depth):
            for n in ("ina", "inb", "uq", "uk", "uv", "pc", "pm", "o"):
                b = dbuf()
                b._keep = True
                dw[(n, l)] = b
        d_x0 = dbuf()

        ps = [Buf(stack.enter_context(nc.psum_tensor("ps%d" % i, [128, 512], F32)), excl=True) for i in range(8)]
        rot = {"i": 0}

        def psg(n=5):
            b = ps[rot["i"] % n]
            rot["i"] += 1
            return b

        ident_f = sbk(stack, "ident_f", [128, 193], F32)
        ident_b = sbk(stack, "ident_b", [128, 128], BF16)
        Rm_b = sbk(stack, "Rm_b", [64, 64], BF16)
        ones_b = sbk(stack, "ones_b", [128, 128], BF16)
        ones_d = sbk(stack, "ones_d", [128, 128], BF16)
        ones_q = sbk(stack, "ones_q", [128, 128], BF16)
        ones_f = sbk(stack, "ones_f", [128, 128], F32)
        negpi = sbk(stack, "negpi", [128, 1], F32)
        epsb = sbk(stack, "epsb", [128, 2], F32)
        pvec = sbk(stack, "pvec", [128, 5, depth, NCH], F32)
        lnb_h = sbk(stack, "lnb_h", [128, depth, NCH], F32)
        cwT = sbk(stack, "cwT", [128, depth, NCH, CW], F32)
        lorag = sbk(stack, "lorag", [128, 2, depth, 4], F32)
        hn_n = sbk(stack, "hn_n", [128, 2, depth], F32)
        hn_r = sbk(stack, "hn_r", [64, 2, depth], F32)

        for dst, src in ((ident_f, cst_in), (pvec, pv_in), (cwT, cw_in), (lorag, lg_in), (hn_n, hn_in), (hn_r, hr_in)):
            sch.dma("sp", dst[:], src[:], dst, writes=[dst])
        sch.op("dve", lambda e: e.tensor_copy(ident_b[:], ident_f[:, 0:128]), [ident_f], [ident_b])
        sch.op("dve", lambda e: e.tensor_copy(Rm_b[:], ident_f[0:64, 128:192]), [ident_f], [Rm_b])
        sch.op("pool", lambda e: e.memset(ones_b[:], 1.0), [], [ones_b])
        sch.op("pool", lambda e: e.memset(ones_d[:], 1.0 / 2048.0), [], [ones_d])
        sch.op("pool", lambda e: e.memset(ones_q[:], 1.0 / 512.0), [], [ones_q])
        sch.op("pool", lambda e: e.memset(ones_f[:], 1.0), [], [ones_f])
        sch.op("pool", lambda e: e.memset(negpi[:], -PI), [], [negpi])
        sch.op("pool", lambda e: e.memset(epsb[:, 0:1], EPS), [], [epsb])
        sch.op("pool", lambda e: e.memset(epsb[:, 1:2], 192.0 * EPS), [], [epsb])

        def rsq(out_ap, out_buf, in_ap, in_buf, tmp_ap, tmp_buf, power=-0.5, eps_col=0, rows=128):
            sch.op("act", lambda e: e.activation(out=tmp_ap, in_=in_ap, func=AF.Ln, bias=epsb[0:rows, eps_col:eps_col + 1], scale=1.0),
                   [in_buf, epsb], [tmp_buf])
            sch.op("act", lambda e: e.activation(out=out_ap, in_=tmp_ap, func=AF.Exp, scale=power), [tmp_buf], [out_buf])


        cast_jobs = []

        def cast(dst, src, key, rows, cols, rb=512, cb=2048, cbeg=0):
            for r0 in range(0, rows, rb):
                for c0 in range(cbeg, cols, cb):
                    c1 = min(cols, c0 + cb)
                    cast_jobs.append(lambda r0=r0, c0=c0, c1=c1: sch.dma(
                        "pool", dst[r0:r0 + rb, c0:c1], src[r0:r0 + rb, c0:c1], dw[key], writes=[dw[key]]))

        def cast_kv(l):
            kv4 = w_ukv[l].rearrange("k (h two d) -> k h two d", two=2, d=128)
            for r0 in range(0, 512, 128):
                cast_jobs.append(lambda r0=r0: sch.dma("pool", wb_uk[l][r0:r0 + 128], kv4[r0:r0 + 128, :, 0, :], dw[("uk", l)],
                                                       writes=[dw[("uk", l)]]))
                cast_jobs.append(lambda r0=r0: sch.dma("pool", wb_uv[l][r0:r0 + 128], kv4[r0:r0 + 128, :, 1, :], dw[("uv", l)],
                                                       writes=[dw[("uv", l)]]))

        n_upfront = None
        for l in range(depth):
            cast(wb_in[l], w_in[l], ("ina", l), D, OFF_QD)
            cast(wb_in[l], w_in[l], ("inb", l), D, IN_DIM, cbeg=OFF_QD)
            cast(wb_uq[l], w_uq[l], ("uq", l), 512, 3072)
            cast_kv(l)
            if n_upfront is None:
                n_upfront = len(cast_jobs)
            cast(wb_pc[l], w_pc[l], ("pc", l), D, D)
            cast(wb_pm[l], w_pm[l], ("pm", l), D, D)
            cast(wb_o[l], w_o[l], ("o", l), D, D)

        def emit_casts(n):
            for _ in range(min(n, len(cast_jobs))):
                cast_jobs.pop(0)()

        with ExitStack() as st:
            posi = sb(st, "posi", [64, S], I32)
            ang = sb(st, "ang", [64, S], F32)
            tmp = sb(st, "tmpT", [64, S], F32)
            tab = sb(st, "tabT", [64, S], F32)
            pos_b = pos_in.to_broadcast([64, S])
            sch.dma("sp", posi[:], pos_b, posi, writes=[posi])
            sch.op("dve", lambda e: e.tensor_copy(ang[:], posi[:]), [posi], [ang])
            sch.op("dve", lambda e: e.tensor_scalar(ang[:], ang[:], ident_f[0:64, 192:193], None, ALU.mult), [ang, ident_f], [ang])
            for shift, dstT in ((0.0, sinT), (0.5 * PI, cosT)):
                sch.op("dve", lambda e: e.tensor_scalar(tmp[:], ang[:], shift, 1.0 / (2.0 * PI), ALU.add, ALU.mult), [ang], [tmp])
                sch.op("dve", lambda e: e.tensor_copy(posi[:], tmp[:]), [tmp], [posi])
                sch.op("dve", lambda e: e.tensor_copy(tmp[:], posi[:]), [posi], [tmp])
                sch.op("dve", lambda e: e.scalar_tensor_tensor(out=tmp[:], in0=tmp[:], scalar=-2.0 * PI, in1=ang[:],
                                                               op0=ALU.mult, op1=ALU.add), [tmp, ang], [tmp])
                sch.op("dve", lambda e: e.tensor_scalar(tmp[:], tmp[:], shift, None, ALU.add), [tmp], [tmp])
                sch.op("dve", lambda e: e.tensor_scalar(tmp[:], tmp[:], -PI, PI, ALU.max, ALU.min), [tmp], [tmp])
                sch.op("act", lambda e: e.activation(out=tab[:], in_=tmp[:], func=AF.Sin), [tmp], [tab])
                sch.dma("pool", dstT[:], tab[:], tab, reads=[tab], writes=[dr["cos"]])
            sch.barrier(skip_keep=True)
        emit_casts(n_upfront)

        def wview(wb, c0, c1):
            return wb.rearrange("(k p) n -> p k n", p=128)[:, :, c0:c1]

        def mm16(pbuf, wblk, coff, m, rhs_buf, rhs_fn, extra_reads=(), nk=NCH, mrows=128):
            def fn(e):
                ins = None
                for k in range(nk):
                    ins = e.matmul(pbuf[0:m, :] if m < 128 else pbuf[:], wblk[:, k, coff:coff + m], rhs_fn(k),
                                   start=(k == 0), stop=(k == nk - 1))
                return ins
            return sch.op("pe", fn, [wblk, rhs_buf] + list(extra_reads), [pbuf])

        def phase_A(l, xsrc, xsrc_buf):
            if l > 0:
                emit_casts(len(cast_jobs))
            g_norm = lambda c: pvec[:, 0, l, c:c + 1]
            g_cb = lambda c: pvec[:, 1, l, c:c + 1]
            g_lg = lambda c: pvec[:, 2, l, c:c + 1]
            g_lb = lambda c: pvec[:, 3, l, c:c + 1]
            ntile = NTILE if ntile_limit is None else ntile_limit
            with ExitStack() as st:
                xt = [sb(st, "xt%d" % i, [128, TT], F32) for i in range(2)]
                xr = {"i": 0}
                hT = [sb(st, "hT%d" % i, [128, NCH, TT], BF16) for i in range(2)]
                sq = [sb(st, "sq%d" % i, [128, TT], BF16) for i in range(2)]
                rstd = sb(st, "rstd", [128, TT], F32)
                lnt = sb(st, "lnt", [128, TT], F32)
                wblk = [sb(st, "wblk%d" % i, [128, NCH, 256], BF16) for i in range(4)]
                wr = {"i": 0}
                vst = sb(st, "vst", [128, NCH, TT], BF16)
                uall = sb(st, "uall", [128, NCH, TT + 32], BF16)
                vstb = [Buf(vst.t) for _ in range(NCH)]
                uallb = [Buf(uall.t) for _ in range(NCH)]
                dg = [sb(st, "dg%d" % i, [128, CW, 128], BF16) for i in range(2)]
                tg = [sb(st, "tg%d" % i, [128, TT], F32) for i in range(1)]
                cacc = [sb(st, "cacc%d" % i, [128, TT], F32) for i in range(2)]
                v2 = [sb(st, "v2%d" % i, [128, TT], BF16) for i in range(2)]
                lnA = sb(st, "lnA", [128, TT], F32)
                lnB = sb(st, "lnB", [128, TT], F32)
                lnm = sb(st, "lnm", [128, TT], F32)
                sgt = [sb(st, "sgt%d" % i, [128, TT], F32) for i in range(1)]
                zt = [sb(st, "zt%d" % i, [128, TT], F32) for i in range(1)]
                szt = [sb(st, "szt%d" % i, [128, TT], F32) for i in range(1)]
                stg = [sb(st, "stg%d" % i, [128, TT], BF16) for i in range(4)]
                sr = {"i": 0}
                qd = sb(st, "qd", [128, 4, TT], BF16)
                qsq = sb(st, "qsq", [128, 4, TT], BF16)
                kvd = sb(st, "kvd", [128, 4, TT], BF16)
                kvsq = sb(st, "kvsq", [128, 4, TT], BF16)
                Rq = sb(st, "Rq", [128, TT], F32)
                Rq2 = sb(st, "Rq2", [128, TT], F32)
                Sk = sb(st, "Sk", [64, TT], F32)
                rkv_tm = sb(st, "rkv_tm", [128, 4], F32)
                r2kv_tm = sb(st, "r2kv_tm", [128, 4], F32)
                sspe_tm = sb(st, "sspe_tm", [128, 4], F32)
                fk_tm = sb(st, "fk_tm", [128, 4, NH], F32)
                cs_t = sb(st, "cs_t", [64, 2, TT], F32)
                kpsq = sb(st, "kpsq", [64, TT], BF16)
                rf = [sb(st, "rf%d" % i, [64, TT], F32) for i in range(2)]
                rb_ = [sb(st, "rb%d" % i, [64, TT], BF16) for i in range(2)]
                r1 = [sb(st, "r1%d" % i, [64, TT], F32) for i in range(2)]
                r2 = [sb(st, "r2%d" % i, [64, TT], F32) for i in range(2)]
                Ft = [sb(st, "Ft%d" % i, [128, TT], F32) for i in range(2)]
                qsn = [sb(st, "qsn%d" % i, [128, TT], BF16) for i in range(2)]
                qsr = [sb(st, "qsr%d" % i, [64, TT], BF16) for i in range(2)]
                ksq = [sb(st, "ksq%d" % i, [128, TT], BF16) for i in range(2)]
                stg64 = [sb(st, "stg64_%d" % i, [64, TT], BF16) for i in range(2)]
                s64 = {"i": 0}

                def next_stage():
                    b = stg[sr["i"] % len(stg)]
                    sr["i"] += 1
                    return b

                def next_stage64():
                    b = stg64[s64["i"] % len(stg64)]
                    s64["i"] += 1
                    return b

                def load_w(wb, key, c0, ncols, nk=NCH):
                    if key is None:
                        key = ("ina", l) if c0 < OFF_QD else ("inb", l)
                    b = wblk[wr["i"] % len(wblk)]
                    wr["i"] += 1
                    sch.dma("sp", b[:, 0:nk, 0:ncols], wview(wb, c0, c0 + ncols), b, reads=[dw[key]], writes=[b])
                    return b

                sch.op("pool", lambda e: e.memset(uall[:], 0.0), [], uallb)
                pS1, pS2, pT = ps[5], ps[6], ps[7]

                def a0_steps(jn):
                    tsn = slice(jn * TT, (jn + 1) * TT)
                    hn = hT[jn % 2]
                    pss = ps[7]

                    def load_x(c):
                        x_t = xt[xr["i"] % len(xt)]
                        xr["i"] += 1
                        sch.dma("sp", x_t[:], xsrc[c * 128:(c + 1) * 128, tsn], x_t, reads=[xsrc_buf], writes=[x_t])
                        return x_t
                    steps = []

                    def st1(c):
                        if c < NCH:
                            x_t = load_x(c)
                            s_ = sq[c % 2]
                            sch.op("act", lambda e: e.activation(out=s_[:], in_=x_t[:], func=AF.Square), [x_t], [s_])
                        if c >= 1:
                            p_ = sq[(c - 1) % 2]
                            sch.op("pe", lambda e: e.matmul(pss[:], ones_d[:], p_[:], start=(c - 1 == 0), stop=(c - 1 == NCH - 1)),
                                   [ones_d, p_], [pss])

                    def st2():
                        rsq(rstd[:], rstd, pss[:], pss, lnt[:], lnt)

                    def st3(c):
                        x_t = load_x(c)
                        sch.op("dve", lambda e: e.scalar_tensor_tensor(out=hn[:, c, :], in0=x_t[:], scalar=g_norm(c),
                                                                       in1=rstd[:], op0=ALU.mult, op1=ALU.mult),
                               [x_t, rstd, pvec], [hn])
                    for c in range(NCH + 1):
                        steps.append(lambda c=c: st1(c))
                    steps.append(st2)
                    for c in range(NCH):
                        steps.append(lambda c=c: st3(c))
                    return steps

                for f_ in a0_steps(0):
                    f_()

                for j in range(ntile):
                    tsl = slice(j * TT, (j + 1) * TT)
                    h = hT[j % 2]
                    emit_casts(2)
                    a1w = {}

                    def a1_s1(m):
                        mb, mi = m // 2, m % 2
                        if mi == 0:
                            a1w["wa"] = load_w(wb_in[l], None, OFF_A + mb * 256, 256)
                            a1w["wg"] = load_w(wb_in[l], None, OFF_GLU + mb * 256, 256)
                        wa, wg = a1w["wa"], a1w["wg"]
                        pA, pG = ps[(m % 2) * 2], ps[(m % 2) * 2 + 1]
                        mm16(pA, wa, mi * 128, 128, h, lambda k: h[:, k, :])
                        mm16(pG, wg, mi * 128, 128, h, lambda k: h[:, k, :])
                        t_ = tg[0]
                        sch.op("act", lambda e: e.activation(out=t_[:], in_=pG[:], func=AF.Tanh, scale=0.5), [pG], [t_])
                        if j > 0:
                            sch.op("pool", lambda e: e.tensor_copy(uall[:, m, 0:30], uall[:, m, TT:TT + 30]), [uallb[m]], [uallb[m]])
                        sch.op("dve", lambda e: e.scalar_tensor_tensor(out=uall[:, m, 30:30 + TT], in0=t_[:], scalar=1.0,
                                                                       in1=pA[:], op0=ALU.add, op1=ALU.mult),
                               [t_, pA], [uallb[m]])
                        d_ = dg[m % 2]
                        in0 = ident_b[:, None, :].to_broadcast([128, CW, 128])
                        in1 = cwT[:, l, m, :, None].to_broadcast([128, CW, 128])
                        sch.op("pool", lambda e: e.tensor_tensor(d_[:], in0, in1, ALU.mult), [ident_b, cwT], [d_])
                        acc = cacc[m % 2]
                        for k in range(DVE_TAPS):
                            if k == 0:
                                sch.op("dve", lambda e: e.tensor_scalar(acc[:], uall[:, m, k:k + TT], cwT[:, l, m, k:k + 1], None, ALU.mult),
                                       [uallb[m], cwT], [acc])
                            else:
                                sch.op("dve", lambda e: e.scalar_tensor_tensor(out=acc[:], in0=uall[:, m, k:k + TT], scalar=cwT[:, l, m, k:k + 1],
                                                                               in1=acc[:], op0=ALU.mult, op1=ALU.add),
                                       [uallb[m], cwT, acc], [acc])

                    def a1_s2(m):
                        d_ = dg[m % 2]
                        pC = ps[4] if m % 2 == 0 else ps[7]

                        def convmm(e):
                            ins = None
                            for k in range(DVE_TAPS, CW):
                                ins = e.matmul(pC[:], d_[:, k, :], uall[:, m, k:k + TT], start=(k == DVE_TAPS), stop=(k == CW - 1))
                            return ins
                        sch.op("pe", convmm, [d_, uallb[m]], [pC])
                        if DVE_TAPS > 0:
                            acc = cacc[m % 2]
                            sch.op("dve", lambda e: e.tensor_tensor(acc[:], pC[:], acc[:], ALU.add), [pC, acc], [acc])
                            sch.op("act", lambda e: e.activation(out=vst[:, m, :], in_=acc[:], func=AF.Identity,
                                                                 bias=g_cb(m), scale=0.5), [acc, pvec], [vstb[m]])
                        else:
                            sch.op("act", lambda e: e.activation(out=vst[:, m, :], in_=pC[:], func=AF.Identity,
                                                                 bias=g_cb(m), scale=0.5), [pC, pvec], [vstb[m]])
                        q_ = v2[m % 2]
                        sch.op("act", lambda e: e.activation(out=q_[:], in_=vst[:, m, :], func=AF.Square), [vstb[m]], [q_])

                    def a1_s3(m):
                        q_ = v2[m % 2]
                        sch.op("pe", lambda e: e.matmul(pS1[:], ones_d[:], vst[:, m, :], start=(m == 0), stop=(m == NCH - 1)),
                               [ones_d, vstb[m]], [pS1])
                        sch.op("pe", lambda e: e.matmul(pS2[:], ones_d[:], q_[:], start=(m == 0), stop=(m == NCH - 1)),
                               [ones_d, q_], [pS2])

                    if "A1" in A_PARTS:
                        for i in range(NCH + 2):
                            if i < NCH:
                                a1_s1(i)
                            if 0 <= i - 1 < NCH:
                                a1_s2(i - 1)
                            if 0 <= i - 2 < NCH:
                                a1_s3(i - 2)
                    if "A1" not in A_PARTS:
                        sch.op("pe", lambda e: e.matmul(pS1[:], ones_d[:], ones_d[:, 0:1].to_broadcast([128, 512]) if False else sq[0][:], start=True, stop=True), [ones_d, sq[0]], [pS1])
                        sch.op("pe", lambda e: e.matmul(pS2[:], ones_d[:], sq[0][:], start=True, stop=True), [ones_d, sq[0]], [pS2])
                    sch.op("act", lambda e: e.activation(out=lnm[:], in_=pS1[:], func=AF.Identity), [pS1], [lnm])
                    sch.op("dve", lambda e: e.tensor_tensor(lnB[:], lnm[:], lnm[:], ALU.mult), [lnm], [lnB])
                    sch.op("dve", lambda e: e.tensor_tensor(lnA[:], pS2[:], lnB[:], ALU.subtract), [pS2, lnB], [lnA])
                    sch.op("dve", lambda e: e.tensor_scalar(lnA[:], lnA[:], 0.0, None, ALU.max), [lnA], [lnA])
                    rsq(lnA[:], lnA, lnA[:], lnA, lnt[:], lnt)
                    sch.op("dve", lambda e: e.scalar_tensor_tensor(out=lnB[:], in0=lnm[:], scalar=-1.0, in1=lnA[:],
                                                                   op0=ALU.mult, op1=ALU.mult), [lnm, lnA], [lnB])
                    for mb in (range(8) if "A1" in A_PARTS else ()):
                        wc = load_w(wb_in[l], None, OFF_CG + mb * 256, 256)
                        for mi in range(2):
                            m = mb * 2 + mi
                            pA = psg()
                            mm16(pA, wc, mi * 128, 128, h, lambda k: h[:, k, :])
                            s_ = sgt[0]
                            sch.op("act", lambda e: e.activation(out=s_[:], in_=pA[:], func=AF.Silu), [pA], [s_])
                            z_ = zt[0]
                            sch.op("dve", lambda e: e.tensor_tensor(z_[:], vst[:, m, :], lnA[:], ALU.mult), [vstb[m], lnA], [z_])
                            sch.op("dve", lambda e: e.tensor_tensor(z_[:], z_[:], lnB[:], ALU.add), [z_, lnB], [z_])
                            y_ = szt[0]
                            sch.op("act", lambda e: e.activation(out=y_[:], in_=z_[:], func=AF.Silu, bias=g_lb(m), scale=g_lg(m)),
                                   [z_, pvec], [y_])
                            o_ = next_stage()
                            sch.op("dve", lambda e: e.tensor_tensor(o_[:], y_[:], s_[:], ALU.mult), [y_, s_], [o_])
                            sch.dma("pool", cbrT[m * 128:(m + 1) * 128, tsl], o_[:], o_, reads=[o_], writes=[dr["cbr"]])

                    gate_tasks = []
                    if "A2" in A_PARTS:
                        for off, func, scale, dst, key in ((OFF_MG, AF.Silu, 1.0, smgT, "smg"), (OFF_GC, AF.Tanh, 0.5, tgcT, "tgc"),
                                                           (OFF_GM, AF.Tanh, 0.5, tgmT, "tgm")):
                            for m in range(NCH):
                                gate_tasks.append((off, func, scale, dst, key, m))
                    gw = {}
                    gcnt = {"i": 0}

                    def run_gate(fixed_banks=True):
                        if not gate_tasks:
                            return
                        off, func, scale, dst, key, m = gate_tasks.pop(0)
                        mb, mi = m // 2, m % 2
                        if mi == 0:
                            gw["w"] = load_w(wb_in[l], None, off + mb * 256, 256)
                        wc = gw["w"]
                        if fixed_banks:
                            pA = ps[5 + gcnt["i"] % 2]
                            gcnt["i"] += 1
                        else:
                            pA = psg()
                        mm16(pA, wc, mi * 128, 128, h, lambda k: h[:, k, :])
                        o_ = next_stage()
                        sch.op("act", lambda e: e.activation(out=o_[:], in_=pA[:], func=func, scale=scale), [pA], [o_])
                        sch.dma("pool", dst[m * 128:(m + 1) * 128, tsl], o_[:], o_, reads=[o_], writes=[dr[key]])

                    def finish_tile():
                        nxt = a0_steps(j + 1) if j + 1 < ntile else []
                        while gate_tasks or nxt:
                            run_gate(fixed_banks=False)
                            for _ in range(3):
                                if nxt:
                                    nxt.pop(0)()

                    if "A3" not in A_PARTS:
                        finish_tile()
                        continue
                    sch.dma("sp", cs_t[:, 0, :], cosT[:, tsl], cs_t, reads=[dr["cos"]], writes=[cs_t])
                    sch.dma("sp", cs_t[:, 1, :], sinT[:, tsl], cs_t, reads=[dr["cos"]], writes=[cs_t])
                    for (off, dstd, dsq, gi) in ((OFF_QD, qd, qsq, 0), (OFF_KVD, kvd, kvsq, 1)):
                        for c in range(4):
                            if c % 2 == 0:
                                wc = load_w(wb_in[l], None, off + c * 128, 256)
                            pA = psg()
                            mm16(pA, wc, (c % 2) * 128, 128, h, lambda k: h[:, k, :])
                            sch.op("act", lambda e: e.activation(out=dsq[:, c, :], in_=pA[:], func=AF.Square), [pA], [dsq])
                            sch.op("dve", lambda e: e.tensor_scalar(dstd[:, c, :], pA[:], lorag[:, gi, l, c:c + 1], None, ALU.mult),
                                   [pA, lorag], [dstd])
                    pq = psg()
                    sch.op("pe", lambda e: [e.matmul(pq[:], ones_q[:], qsq[:, c, :], start=(c == 0), stop=(c == 3)) for c in range(4)][-1],
                           [ones_q, qsq], [pq])
                    rsq(Rq[:], Rq, pq[:], pq, lnt[:], lnt)
                    sch.op("dve", lambda e: e.scalar_tensor_tensor(out=Rq2[:], in0=Rq[:], scalar=1.0 / 192.0, in1=Rq[:],
                                                                   op0=ALU.mult, op1=ALU.mult), [Rq], [Rq2])
                    pk = psg()
                    sch.op("pe", lambda e: [e.matmul(pk[:], ones_q[:], kvsq[:, c, :], start=(c == 0), stop=(c == 3)) for c in range(4)][-1],
                           [ones_q, kvsq], [pk])
                    rsq(Sk[:], Sk, pk[0:64, :], pk, lnt[0:64, :], lnt, power=0.5, rows=64)
                    def kvstat(e):
                        ins = None
                        for tb in range(4):
                            for c in range(4):
                                ins = e.matmul(pT[:, tb:tb + 1], kvsq[:, c, tb * 128:(tb + 1) * 128], ones_q[:, 0:1],
                                               start=(c == 0), stop=(c == 3))
                        return ins
                    sch.op("pe", kvstat, [kvsq, ones_q], [pT])
                    rsq(rkv_tm[:], rkv_tm, pT[:, 0:4], pT, lnt[:, 0:4], lnt)
                    sch.op("dve", lambda e: e.tensor_tensor(r2kv_tm[:], rkv_tm[:], rkv_tm[:], ALU.mult), [rkv_tm], [r2kv_tm])

                    if A3_STOP <= 1:
                        finish_tile()
                        continue
                    wc = load_w(wb_in[l], None, OFF_KPE, 64)
                    pK = psg()
                    mm16(pK, wc, 0, 64, h, lambda k: h[:, k, :])
                    sch.op("act", lambda e: e.activation(out=kpsq[:], in_=pK[0:64, :], func=AF.Square), [pK], [kpsq])

                    def pestat(e):
                        ins = None
                        for tb in range(4):
                            ins = e.matmul(pT[:, 4 + tb:5 + tb], kpsq[:, tb * 128:(tb + 1) * 128], ones_b[0:64, 0:1],
                                           start=True, stop=True)
                        return ins
                    sch.op("pe", pestat, [kpsq, ones_b], [pT])
                    sch.op("dve", lambda e: e.tensor_copy(sspe_tm[:], pT[:, 4:8]), [pT], [sspe_tm])

                    def rope(src_f, src_b, out_ap, out_buf, extra_mul=None, extra_buf=None, ri=0, pR=None):
                        pR = psg() if pR is None else pR
                        sch.op("pe", lambda e: e.matmul(pR[0:64, :], Rm_b[:], src_b[:], start=True, stop=True), [Rm_b, src_b], [pR])
                        a_, b_ = r1[ri], r2[ri]
                        sch.op("pool", lambda e: e.tensor_tensor(a_[:], src_f[:], cs_t[:, 0, :], ALU.mult), [src_f, cs_t], [a_])
                        sch.op("dve", lambda e: e.tensor_tensor(b_[:], pR[0:64, :], cs_t[:, 1, :], ALU.mult), [pR, cs_t], [b_])
                        if extra_mul is None:
                            sch.op("pool", lambda e: e.tensor_tensor(out_ap, a_[:], b_[:], ALU.add), [a_, b_], [out_buf])
                        else:
                            sch.op("pool", lambda e: e.tensor_tensor(a_[:], a_[:], b_[:], ALU.add), [a_, b_], [a_])
                            sch.op("pool", lambda e: e.tensor_tensor(out_ap, a_[:], extra_mul, ALU.mult), [a_, extra_buf], [out_buf])

                    sch.op("dve", lambda e: e.tensor_scalar(rf[0][:], pK[0:64, :], hn_r[:, 1, l:l + 1], None, ALU.mult), [pK, hn_r], [rf[0]])
                    sch.op("pool", lambda e: e.tensor_copy(rb_[0][:], rf[0][:]), [rf[0]], [rb_[0]])
                    o64 = next_stage64()
                    rope(rf[0], rb_[0], o64[:], o64, extra_mul=Sk[:], extra_buf=Sk, ri=0)
                    sch.dma("pool", kTr[:, tsl], o64[:], o64, reads=[o64], writes=[dr["kr"]])

                    if A3_STOP <= 2:
                        finish_tile()
                        continue
                    qw = {}

                    def q_s1(hh):
                        hg, hi = hh // 4, hh % 4
                        if hi == 0:
                            b = wblk[wr["i"] % len(wblk)]
                            wr["i"] += 1
                            wq3 = b[:].rearrange("p k n -> p (k n)")[:, 0:4 * 768].rearrange("p (k n) -> p k n", k=4)
                            sch.dma("sp", wq3, wb_uq[l].rearrange("(k p) n -> p k n", p=128)[:, :, hg * 768:(hg + 1) * 768], b,
                                    reads=[dw[("uq", l)]], writes=[b])
                            qw["b"], qw["w"] = b, wq3
                        b, wq3 = qw["b"], qw["w"]
                        ri = hh % 2
                        pQn, pQr = ps[ri * 2], ps[ri * 2 + 1]
                        sch.op("pe", lambda e: [e.matmul(pQn[:], wq3[:, c, hi * 192:hi * 192 + 128], qd[:, c, :],
                                                         start=(c == 0), stop=(c == 3)) for c in range(4)][-1], [b, qd], [pQn])
                        sch.op("pe", lambda e: [e.matmul(pQr[0:64, :], wq3[:, c, hi * 192 + 128:hi * 192 + 192], qd[:, c, :],
                                                         start=(c == 0), stop=(c == 3)) for c in range(4)][-1], [b, qd], [pQr])
                        sn, sr_ = qsn[ri], qsr[ri]
                        sch.op("act", lambda e: e.activation(out=sn[:], in_=pQn[:], func=AF.Square), [pQn], [sn])
                        sch.op("act", lambda e: e.activation(out=sr_[:], in_=pQr[0:64, :], func=AF.Square), [pQr], [sr_])

                    def q_s2(hh):
                        ri = hh % 2
                        pQn, pQr = ps[ri * 2], ps[ri * 2 + 1]
                        sn, sr_ = qsn[ri], qsr[ri]
                        pSS = ps[4]

                        def ssmm(e):
                            e.matmul(pSS[:], ones_b[:], sn[:], start=True, stop=False)
                            return e.matmul(pSS[:], ones_b[0:64, :], sr_[:], start=False, stop=True)
                        sch.op("pe", ssmm, [ones_b, sn, sr_], [pSS])
                        F_ = Ft[ri]
                        sch.op("dve", lambda e: e.tensor_tensor(F_[:], pSS[:], Rq2[:], ALU.mult), [pSS, Rq2], [F_])
                        rsq(F_[:], F_, F_[:], F_, lnt[:], lnt)
                        sch.op("dve", lambda e: e.tensor_tensor(F_[:], F_[:], Rq[:], ALU.mult), [F_, Rq], [F_])
                        o_ = next_stage()
                        sch.op("dve", lambda e: e.scalar_tensor_tensor(out=o_[:], in0=pQn[:], scalar=hn_n[:, 0, l:l + 1], in1=F_[:],
                                                                       op0=ALU.mult, op1=ALU.mult), [pQn, hn_n, F_], [o_])
                        sch.dma("pool", qTn[hh, :, tsl], o_[:], o_, reads=[o_], writes=[dr["qn"]])
                        sch.op("dve", lambda e: e.scalar_tensor_tensor(out=rf[ri][:], in0=pQr[0:64, :], scalar=hn_r[:, 0, l:l + 1],
                                                                       in1=F_[0:64, :], op0=ALU.mult, op1=ALU.mult),
                               [pQr, hn_r, F_], [rf[ri]])
                        sch.op("act", lambda e: e.activation(out=rb_[ri][:], in_=rf[ri][:], func=AF.Identity), [rf[ri]], [rb_[ri]])

                    def q_s3(hh):
                        ri = hh % 2
                        o64 = next_stage64()
                        rope(rf[ri], rb_[ri], o64[:], o64, ri=ri, pR=ps[7])
                        sch.dma("pool", qTr[hh, :, tsl], o64[:], o64, reads=[o64], writes=[dr["qr"]])

                    for i in range(NH + 2):
                        if i < NH:
                            q_s1(i)
                        if 0 <= i - 1 < NH:
                            q_s2(i - 1)
                        if 0 <= i - 2 < NH:
                            q_s3(i - 2)
                        run_gate()

                    if A3_STOP <= 3:
                        finish_tile()
                        continue
                    pF = ps[7]
                    kw = {}

                    def k_s1(hh):
                        if hh % 4 == 0:
                            bk = wblk[wr["i"] % len(wblk)]
                            wr["i"] += 1
                            wk3 = bk[:].rearrange("p k n -> p (k n)")[:, 0:4 * 512].rearrange("p (k n) -> p k n", k=4)
                            sch.dma("sp", wk3, wb_uk[l].rearrange("(k p) h d -> p k (h d)", p=128)[:, :, (hh // 4) * 512:(hh // 4 + 1) * 512],
                                    bk, reads=[dw[("uk", l)]], writes=[bk])
                            kw["b"], kw["w"] = bk, wk3
                        bk, wk3 = kw["b"], kw["w"]
                        pKn = psg()
                        sch.op("pe", lambda e: [e.matmul(pKn[:], wk3[:, c, (hh % 4) * 128:(hh % 4 + 1) * 128], kvd[:, c, :],
                                                         start=(c == 0), stop=(c == 3)) for c in range(4)][-1], [bk, kvd], [pKn])
                        s_ = ksq[hh % 2]
                        sch.op("act", lambda e: e.activation(out=s_[:], in_=pKn[:], func=AF.Square), [pKn], [s_])
                        o_ = next_stage()
                        sch.op("dve", lambda e: e.tensor_scalar(o_[:], pKn[:], hn_n[:, 1, l:l + 1], None, ALU.mult), [pKn, hn_n], [o_])
                        sch.dma("pool", kTn[hh, :, tsl], o_[:], o_, reads=[o_], writes=[dr["kn"]])

                    def k_s2(hh):
                        s_ = ksq[hh % 2]

                        def kstat(e):
                            ins = None
                            for tb in range(4):
                                ins = e.matmul(pF[:, 8 + tb * NH + hh:9 + tb * NH + hh], s_[:, tb * 128:(tb + 1) * 128], ones_b[:, 0:1],
                                               start=True, stop=True)
                            return ins
                        sch.op("pe", kstat, [s_, ones_b], [pF])

                    for i in range(NH + 1):
                        if i < NH:
                            k_s1(i)
                        if 0 <= i - 1 < NH:
                            k_s2(i - 1)
                        if i < NH:
                            run_gate()
                    ssn = pF[:, 8:8 + 4 * NH].rearrange("p (t h) -> p t h", t=4)
                    sch.op("dve", lambda e: e.tensor_tensor(fk_tm[:], ssn, r2kv_tm[:, :, None].to_broadcast([128, 4, NH]), ALU.mult),
                           [pF, r2kv_tm], [fk_tm])
                    sch.op("dve", lambda e: e.tensor_tensor(fk_tm[:], fk_tm[:], sspe_tm[:, :, None].to_broadcast([128, 4, NH]), ALU.add),
                           [fk_tm, sspe_tm], [fk_tm])
                    rsq(fk_tm[:], fk_tm, fk_tm[:], fk_tm, lnt[:, 0:4 * NH].rearrange("p (t h) -> p t h", t=4), lnt, eps_col=1)
                    sch.op("dve", lambda e: e.tensor_tensor(fk_tm[:], fk_tm[:], rkv_tm[:, :, None].to_broadcast([128, 4, NH]), ALU.mult),
                           [fk_tm, rkv_tm], [fk_tm])
                    sch.dma("pool", Fkd[tsl, :].rearrange("(t p) h -> p t h", p=128), fk_tm[:], fk_tm, reads=[fk_tm], writes=[dr["Fk"]])

                    if A3_STOP <= 4:
                        finish_tile()
                        continue
                    for hg in range(4):
                        bv = wblk[wr["i"] % len(wblk)]
                        wr["i"] += 1
                        wv3 = bv[:].rearrange("p k n -> p (k n)")[:, 0:4 * 512].rearrange("p (k n) -> p k n", k=4)
                        sch.dma("sp", wv3, wb_uv[l].rearrange("(k p) h d -> p k (h d)", p=128)[:, :, hg * 512:(hg + 1) * 512], bv,
                                reads=[dw[("uv", l)]], writes=[bv])
                        for tb in range(4):
                            pV = psg()
                            sch.op("pe", lambda e: [e.matmul(pV[:], kvd[:, c, tb * 128:(tb + 1) * 128], wv3[:, c, :],
                                                             start=(c == 0), stop=(c == 3)) for c in range(4)][-1], [bv, kvd], [pV])
                            o_ = next_stage()
                            sch.op("dve", lambda e: e.tensor_scalar(o_[:], pV[:], rkv_tm[:, tb:tb + 1], None, ALU.mult),
                                   [pV, rkv_tm], [o_])
                            sch.dma("pool", Vd[j * TT + tb * 128:j * TT + (tb + 1) * 128, hg * 512:(hg + 1) * 512], o_[:], o_,
                                    reads=[o_], writes=[dr["V"]])
                    finish_tile()
                sch.barrier()

        def phase_B(l):
            ntile = NTILE if ntile_limit is None else ntile_limit
            with ExitStack() as st:
                krope = sb(st, "krope", [128, S], BF16)
                fk_all = sb(st, "fk_all", [128, S // 128, NH], F32)
                kn = [sb(st, "kn%d" % i, [128, S], BF16) for i in range(2)]
                vh = [sb(st, "vh%d" % i, [128, S // 128, 128], BF16) for i in range(2)]
                qn = [sb(st, "qn%d" % i, [128, TT], BF16) for i in range(2)]
                qr = [sb(st, "qr%d" % i, [128, TT], BF16) for i in range(2)]
                accD = [sb(st, "accD%d" % i, [128, TT], F32) for i in range(2)]
                sg = [sb(st, "sg%d" % i, [128, TT], BF16) for i in range(2)]
                pt = [sb(st, "pt%d" % i, [128, TT], BF16) for i in range(4)]
                rden = [sb(st, "rden%d" % i, [128, TT], F32) for i in range(2)]
                ot = [sb(st, "ot%d" % i, [128, TT], F32) for i in range(2)]
                og = [sb(st, "og%d" % i, [128, TT], BF16) for i in range(2)]
                wrm = sb(st, "wrm", [128, TT], BF16)
                sch.op("pool", lambda e: e.memset(wrm[:], 1.0), [], [wrm])

                def warm(n):
                    if WARM_N == 0:
                        return
                    n = WARM_N
                    sch.op("pe", lambda e: [e.matmul(ps[7][:], ones_b[:], wrm[:], start=(i_ == 0), stop=(i_ == n - 1)) for i_ in range(n)][-1],
                           [ones_b, wrm], [ps[7]])
                nkt = ntile * 4
                sch.op("pool", lambda e: e.memset(krope[64:128, :], 0.0), [], [krope])
                for q_ in qr:
                    sch.op("pool", lambda e: e.memset(q_[64:128, :], 0.0), [], [q_])
                sch.dma("sp", krope[0:64, 0:ntile * TT], kTr[:, 0:ntile * TT], krope, reads=[dr["kr"]], writes=[krope])
                sch.dma("sp", fk_all[:, 0:nkt, :], Fkd[0:ntile * TT, :].rearrange("(t p) h -> p t h", p=128), fk_all,
                        reads=[dr["Fk"]], writes=[fk_all])
                it = 0
                pi_ = 0
                for hh in range(NH):
                    k_ = kn[hh % 2]
                    v_ = vh[hh % 2]
                    sch.dma("sp", k_[:, 0:ntile * TT], kTn[hh, :, 0:ntile * TT], k_, reads=[dr["kn"]], writes=[k_])
                    sch.dma("sp", v_[:, 0:nkt, :], Vd[0:ntile * TT, hh * 128:(hh + 1) * 128].rearrange("(t p) d -> p t d", p=128), v_,
                            reads=[dr["V"]], writes=[v_])
                    warm(24)
                    for j in range(ntile):
                        tsl = slice(j * TT, (j + 1) * TT)
                        q_n, q_r, s_g = qn[it % 2], qr[it % 2], sg[it % 2]
                        pO, pD = ps[3 + 2 * (it % 2)], ps[4 + 2 * (it % 2)]
                        sch.dma("sp", q_n[:], qTn[hh, :, tsl], q_n, reads=[dr["qn"]], writes=[q_n])
                        sch.dma("sp", q_r[0:64, :], qTr[hh, :, tsl], q_r, reads=[dr["qr"]], writes=[q_r])
                        sch.dma("sp", s_g[:], smgT[hh * 128:(hh + 1) * 128, tsl], s_g, reads=[dr["smg"]], writes=[s_g])
                        nkc = 4 * (j + 1)
                        if it % 2 == 0:
                            emit_casts(1)
                        aD = accD[it % 2]
                        if DEN_MODE != "pe":
                            sch.op("pool", lambda e: e.memset(aD[:], 0.0), [], [aD])
                        slots = {}

                        def geom(kc):
                            r = kc - 4 * j
                            return r, (128 * r if r > 0 else 0), slice(kc * 128, (kc + 1) * 128)

                        def s_stage(kc):
                            nonlocal pi_
                            r, c0, ksl = geom(kc)
                            pS = ps[pi_ % 3]
                            p_ = pt[pi_ % 4]
                            pi_ += 1
                            slots[kc] = (pS, p_)

                            def smm(e):
                                e.matmul(pS[:, c0:TT], k_[:, ksl], q_n[:, c0:TT], start=True, stop=False)
                                return e.matmul(pS[:, c0:TT], krope[:, ksl], q_r[:, c0:TT], start=False, stop=True)
                            sch.op("pe", smm, [k_, krope, q_n, q_r], [pS])

                        def e_stage(kc):
                            r, c0, ksl = geom(kc)
                            pS, p_ = slots[kc]
                            sch.op("act", lambda e: e.activation(out=p_[:, c0:TT], in_=pS[:, c0:TT], func=AF.Exp,
                                                                 scale=fk_all[:, kc, hh:hh + 1]), [pS, fk_all], [p_])
                            if r >= 0:
                                sch.op("pool", lambda e: e.memset(p_[64:128, c0:c0 + 64], 0.0), [p_], [p_])
                            if DEN_MODE == "dve" or (DEN_MODE == "split" and kc % 2 == 1):
                                sch.op("dve", lambda e: e.tensor_tensor(aD[:, c0:TT], aD[:, c0:TT], p_[:, c0:TT], ALU.add), [aD, p_], [aD])

                        def pv_stage(kc):
                            r, c0, ksl = geom(kc)
                            pS, p_ = slots[kc]
                            def pvmm(e):
                                ins = e.matmul(pO[:, c0:TT], v_[:, kc, :], p_[:, c0:TT], start=(kc == 0), stop=(kc == nkc - 1))
                                if DEN_MODE == "pe" or (DEN_MODE == "split" and kc % 2 == 0):
                                    ins = e.matmul(pD[:, c0:TT], ones_b[:], p_[:, c0:TT], start=(kc == 0),
                                                   stop=(DEN_MODE == "pe" and kc == nkc - 1))
                                return ins
                            sch.op("pe", pvmm, [v_, p_, ones_b], [pO, pD])

                        s_stage(0)
                        s_stage(1)
                        for kc in range(nkc):
                            if kc + 2 < nkc:
                                s_stage(kc + 2)
                            e_stage(kc)
                            pv_stage(kc)
                        if DEN_MODE != "pe":
                            sch.op("pe", lambda e: e.matmul(pD[:], ones_f[:], aD[:], start=(DEN_MODE == "dve"), stop=True),
                                   [ones_f, aD], [pD])
                        rd, o_, g_ = rden[it % 2], ot[it % 2], og[it % 2]
                        sch.op("dve", lambda e: e.reciprocal(rd[:], pD[:]), [pD], [rd])
                        sch.op("dve", lambda e: e.tensor_tensor(o_[:], pO[:], rd[:], ALU.mult), [pO, rd], [o_])
                        sch.op("pool", lambda e: e.tensor_tensor(g_[:], o_[:], s_g[:], ALU.mult), [o_, s_g], [g_])
                        sch.dma("pool", attT[hh * 128:(hh + 1) * 128, tsl], g_[:], g_, reads=[g_], writes=[dr["att"]])
                        it += 1
                sch.barrier()

        def phase_C(l, xsrc, xsrc_buf, xdst, xdst_buf):
            emit_casts(len(cast_jobs))
            ntile = NTILE if ntile_limit is None else ntile_limit
            with ExitStack() as st:
                cb = [sb(st, "cb%d" % i, [128, NCH, TT], BF16) for i in range(2)]
                ab = [sb(st, "ab%d" % i, [128, NCH, TT], BF16) for i in range(2)]
                yb = sb(st, "yb", [128, NCH, TT], BF16)
                wblk = [sb(st, "wc%d" % i, [128, NCH, 512], BF16) for i in range(3)]
                wr = {"i": 0}
                gcs = [sb(st, "gcs%d" % i, [128, TT], BF16) for i in range(3)]
                gms = [sb(st, "gms%d" % i, [128, TT], BF16) for i in range(3)]
                xs = [sb(st, "xs%d" % i, [128, TT], F32) for i in range(3)]
                t1 = [sb(st, "t1%d" % i, [128, TT], F32) for i in range(2)]
                t2 = [sb(st, "t2%d" % i, [128, TT], F32) for i in range(2)]
                xo = [sb(st, "xo%d" % i, [128, TT], F32) for i in range(3)]

                def load_w(wb, key, c0):
                    b = wblk[wr["i"] % len(wblk)]
                    wr["i"] += 1
                    sch.dma("sp", b[:], wview(wb, c0, c0 + 512), b, reads=[dw[key]], writes=[b])
                    return b

                for j in range(ntile):
                    tsl = slice(j * TT, (j + 1) * TT)
                    c_, a_ = cb[j % 2], ab[j % 2]
                    sch.dma("sp", c_[:], cbrT.rearrange("(c p) t -> p c t", p=128)[:, :, tsl], c_, reads=[dr["cbr"]], writes=[c_])
                    sch.dma("sp", a_[:], attT.rearrange("(c p) t -> p c t", p=128)[:, :, tsl], a_, reads=[dr["att"]], writes=[a_])
                    for mb in range(4):
                        wpc = load_w(wb_pc[l], ("pc", l), mb * 512)
                        wpm = load_w(wb_pm[l], ("pm", l), mb * 512)
                        for mi in range(4):
                            m = mb * 4 + mi
                            gc_, gm_ = gcs[m % 3], gms[m % 3]
                            sch.dma("sp", gc_[:], tgcT[m * 128:(m + 1) * 128, tsl], gc_, reads=[dr["tgc"]], writes=[gc_])
                            sch.dma("sp", gm_[:], tgmT[m * 128:(m + 1) * 128, tsl], gm_, reads=[dr["tgm"]], writes=[gm_])
                            pYc, pYm = psg(), psg()
                            mm16(pYc, wpc, mi * 128, 128, c_, lambda k: c_[:, k, :])
                            mm16(pYm, wpm, mi * 128, 128, a_, lambda k: a_[:, k, :])
                            u1, u2 = t1[m % 2], t2[m % 2]
                            sch.op("dve", lambda e: e.scalar_tensor_tensor(out=u1[:], in0=gc_[:], scalar=1.0, in1=pYc[:],
                                                                           op0=ALU.add, op1=ALU.mult), [gc_, pYc], [u1])
                            sch.op("dve", lambda e: e.scalar_tensor_tensor(out=u2[:], in0=gm_[:], scalar=1.0, in1=pYm[:],
                                                                           op0=ALU.add, op1=ALU.mult), [gm_, pYm], [u2])
                            sch.op("pool", lambda e: e.tensor_tensor(yb[:, m, :], u1[:], u2[:], ALU.add), [u1, u2], [yb])
                    for mb in range(4):
                        wo = load_w(wb_o[l], ("o", l), mb * 512)
                        for mi in range(4):
                            m = mb * 4 + mi
                            x_ = xs[m % 3]
                            sch.dma("sp", x_[:], xsrc[m * 128:(m + 1) * 128, tsl], x_, reads=[xsrc_buf], writes=[x_])
                            pO = psg()
                            mm16(pO, wo, mi * 128, 128, yb, lambda k: yb[:, k, :])
                            o_ = xo[m % 3]
                            sch.op("dve", lambda e: e.scalar_tensor_tensor(out=o_[:], in0=pO[:], scalar=0.5, in1=x_[:],
                                                                           op0=ALU.mult, op1=ALU.add), [pO, x_], [o_])
                            sch.dma("pool", xdst[m * 128:(m + 1) * 128, tsl], o_[:], o_, reads=[o_], writes=[xdst_buf])
                sch.barrier()

        for l in range(depth):
            xsrc, xsb = (xT_in, d_x0) if l == 0 else (x1T, dr["x1"])
            xdst, xdb = (y_out, dr["y"]) if l == depth - 1 else (x1T, dr["x1"])
            if "A" in phases:
                phase_A(l, xsrc, xsb)
            if "B" in phases:
                phase_B(l)
            if "C" in phases:
                phase_C(l, xsrc, xsb, xdst, xdb)
        sch.barrier()
    return nc


def _prep_common(inputs, depth=DEPTH):
    f = np.float32

    def pc(v):
        return np.ascontiguousarray(np.asarray(v, f).reshape(depth, NCH, 128).transpose(2, 0, 1))
    pvec = np.stack([pc(inputs["norm_g"][:depth]), pc(inputs["conv_b"][:depth]), pc(inputs["conv_ln_g"][:depth]),
                     pc(inputs["conv_ln_b"][:depth]), pc(inputs["conv_ln_b"][:depth])], axis=1)
    cw = np.asarray(inputs["conv_w"][:depth], f)
    convwT = np.ascontiguousarray(cw.reshape(depth, CW, NCH, 128).transpose(3, 0, 2, 1))

    def p4(v):
        return np.asarray(v, f).reshape(depth, 4, 128).transpose(2, 0, 1)
    lorag = np.ascontiguousarray(np.stack([p4(inputs["q_a_g"][:depth]), p4(inputs["kv_a_g"][:depth])], axis=1))
    qg = np.asarray(inputs["q_norm_g"][:depth], f)
    kg = np.asarray(inputs["k_norm_g"][:depth], f)
    hn_n = np.ascontiguousarray(np.stack([qg[:, :128].T, kg[:, :128].T], axis=1))
    hn_r = np.ascontiguousarray(np.stack([qg[:, 128:].T, kg[:, 128:].T], axis=1))
    consts = np.zeros((128, 193), f)
    consts[:, :128] = np.eye(128, dtype=f)
    Rm = np.zeros((64, 64), f)
    for p in range(32):
        Rm[p + 32, p] = -1.0
        Rm[p, p + 32] = 1.0
    consts[:64, 128:192] = Rm
    inv = (np.float32(10000.0) ** (-(np.arange(0, 64, 2, dtype=np.float32)) / np.float32(64))).astype(f)
    consts[:64, 192] = np.concatenate([inv, inv])
    common = {
        "w_in": np.ascontiguousarray(np.asarray(inputs["w_in"][:depth], f)),
        "w_uq": np.ascontiguousarray(np.asarray(inputs["w_uq"][:depth], f)),
        "w_ukv": np.ascontiguousarray(np.asarray(inputs["w_ukv"][:depth], f)),
        "w_proj_conv": np.ascontiguousarray(np.asarray(inputs["w_proj_conv"][:depth], f)),
        "w_proj_mla": np.ascontiguousarray(np.asarray(inputs["w_proj_mla"][:depth], f)),
        "w_out": np.ascontiguousarray(np.asarray(inputs["w_out"][:depth], f)),
        "pvec": np.ascontiguousarray(pvec), "convwT": convwT, "lorag": lorag, "hnorm_n": hn_n, "hnorm_r": hn_r,
        "consts": consts,
    }
    return common


def kernel(x, positions, norm_g, w_in, conv_w, conv_b, conv_ln_g, conv_ln_b, q_a_g, w_uq, kv_a_g, w_ukv,
           q_norm_g, k_norm_g, w_proj_conv, w_proj_mla, w_out):
    inputs = dict(norm_g=norm_g, w_in=w_in, conv_w=conv_w, conv_b=conv_b, conv_ln_g=conv_ln_g, conv_ln_b=conv_ln_b,
                  q_a_g=q_a_g, w_uq=w_uq, kv_a_g=kv_a_g, w_ukv=w_ukv, q_norm_g=q_norm_g, k_norm_g=k_norm_g,
                  w_proj_conv=w_proj_conv, w_proj_mla=w_proj_mla, w_out=w_out)
    common = _prep_common(inputs)
    x = np.asarray(x, np.float32)
    positions = np.asarray(positions, np.int32)
    n = x.shape[0]
    in_maps = []
    for b in range(n):
        m = dict(common)
        m["xT"] = np.ascontiguousarray(x[b].T)
        m["pos"] = np.ascontiguousarray(positions[b].reshape(1, S))
        in_maps.append(m)
    nc = build_program()
    res = run_bass_kernel_spmd(nc, in_maps, core_ids=list(range(n)))
    out = np.stack([np.ascontiguousarray(np.asarray(r["yT"]).T) for r in res.results], axis=0)
    return out.astype(np.float32)
```

```python
from contextlib import ExitStack
import numpy as np
import concourse.bass as bass
import concourse.mybir as mybir
from concourse.bass_utils import run_bass_kernel_spmd

F32 = mybir.dt.float32
BF16 = mybir.dt.bfloat16
I32 = mybir.dt.int32
AF = mybir.ActivationFunctionType
ALU = mybir.AluOpType

D = 2048
S = 4096
DEPTH = 2
NCH = 16
TT = 512
NTILE = S // TT
CW = 31
NH = 16
IN_DIM = 13376
EPS = 1e-6
OFF_A, OFF_GLU, OFF_CG, OFF_QD, OFF_KVD, OFF_KPE, OFF_MG, OFF_GC, OFF_GM = (
    0, 2048, 4096, 6144, 6656, 7168, 7232, 9280, 11328)
PI = float(np.pi)
A_PARTS = {"A1", "A2", "A3"}
A3_STOP = 9
WARM_N = 0
DEN_MODE = "pe"


class Buf:
    __slots__ = ("t", "w", "r", "ds", "acc", "excl")

    def __init__(self, t, acc=False, excl=False):
        self.excl = excl
        self.t = t
        self.w = {}
        self.r = {}
        self.ds = None
        self.acc = acc

    def __getitem__(self, idx):
        return self.t[idx]


class DSem:
    __slots__ = ("h", "tot", "key")

    def __init__(self, h, key):
        self.h = h
        self.tot = 0
        self.key = key


class Sched:
    def __init__(self, nc, stack, n_dsem=56):
        self.nc = nc
        self.eng = {"pe": nc.tensor, "act": nc.scalar, "dve": nc.vector, "pool": nc.gpsimd, "sp": nc.sync}
        self.sem = {k: stack.enter_context(nc.semaphore("cs_" + k)) for k in ("pe", "act", "dve", "pool")}
        self.cnt = {k: 0 for k in self.sem}
        self.seen = {k: {} for k in self.eng}
        self.all_dsems = [DSem(stack.enter_context(nc.semaphore("ds%d" % i)), "ds%d" % i) for i in range(n_dsem)]
        self.free_dsems = {"pool": list(self.all_dsems[:26]), "sp": list(self.all_dsems[26:])}
        self.ds_kind = {}
        for k_, lst in self.free_dsems.items():
            for d_ in lst:
                self.ds_kind[d_.key] = k_
        self.phase_bufs = []

    def _gather(self, reads, writes):
        deps = {}

        def add(d):
            for k, sv in d.items():
                if k not in deps or deps[k][1] < sv[1]:
                    deps[k] = sv
        for b in reads:
            add(b.w)
            if b.excl:
                add(b.r)
        for b in writes:
            if not b.acc:
                add(b.w)
            add(b.r)
        return deps

    def _wait(self, e, deps):
        seen = self.seen[e]
        for k, (sem, val) in deps.items():
            if seen.get(k, 0) < val:
                self.eng[e].wait_ge(sem, val)
                seen[k] = val

    def _record(self, key, d, reads, writes):
        for b in writes:
            if b.acc:
                b.w[key] = d
            else:
                b.w = {key: d}
                b.r = {}
        for b in reads:
            b.r[key] = d

    def op(self, e, fn, reads=(), writes=(), inc=True):
        deps = self._gather(reads, writes)
        if e == "pe":
            deps.pop("pe", None)
        self._wait(e, deps)
        ins = fn(self.eng[e])
        if inc:
            self.cnt[e] += 1
            ins.then_inc(self.sem[e], 1)
            self._record(e, (self.sem[e], self.cnt[e]), reads, writes)
        return ins

    def dsem_for(self, owner, q):
        if owner.ds is None:
            owner.ds = self.free_dsems[q].pop()
            self.phase_bufs.append(owner)
        assert self.ds_kind[owner.ds.key] == q, "buffer used with both DMA queue kinds"
        return owner.ds

    def dma(self, q, out, in_, owner, reads=(), writes=()):
        deps = self._gather(reads, writes)
        self._wait(q, deps)
        ds = self.dsem_for(owner, q)
        ins = self.eng[q].dma_start(out=out, in_=in_)
        ds.tot += 16
        ins.then_inc(ds.h, 16)
        self._record(ds.key, (ds.h, ds.tot), reads, writes)
        return ins

    def barrier(self, engines=("pe", "act", "dve", "pool", "sp"), skip_keep=False):
        deps = {k: (self.sem[k], self.cnt[k]) for k in self.sem if self.cnt[k] > 0}
        keep_keys = {b.ds.key for b in self.phase_bufs if getattr(b, "_keep", False) and b.ds is not None}
        for ds in self.all_dsems:
            if ds.tot > 0 and not (skip_keep and ds.key in keep_keys):
                deps[ds.key] = (ds.h, ds.tot)
        for e in engines:
            d = dict(deps)
            if e in d:
                pass
            self._wait(e, d)
        for b in self.phase_bufs:
            if not getattr(b, "_keep", False):
                self.free_dsems[self.ds_kind[b.ds.key]].append(b.ds)
                b.ds = None
        self.phase_bufs = [b for b in self.phase_bufs if b.ds is not None]


def build_program(depth=DEPTH, debug=False, phases="ABC", ntile_limit=None):
    nc = bass.Bass("TRN2", target_bir_lowering=False)
    dbg_kind = "ExternalOutput" if debug else "Internal"

    def din(name, shape, dt=F32):
        return nc.dram_tensor(name, list(shape), dt, kind="ExternalInput").ap()

    def dscr(name, shape, dt, dbg=True):
        return nc.dram_tensor(name, list(shape), dt, kind=(dbg_kind if dbg else "Internal")).ap()

    xT_in = din("xT", [D, S])
    pos_in = din("pos", [1, S], I32)
    w_in = din("w_in", [depth, D, IN_DIM])
    w_uq = din("w_uq", [depth, 512, 3072])
    w_ukv = din("w_ukv", [depth, 512, 4096])
    w_pc = din("w_proj_conv", [depth, D, D])
    w_pm = din("w_proj_mla", [depth, D, D])
    w_o = din("w_out", [depth, D, D])
    pv_in = din("pvec", [128, 5, depth, NCH])
    cw_in = din("convwT", [128, depth, NCH, CW])
    lg_in = din("lorag", [128, 2, depth, 4])
    hn_in = din("hnorm_n", [128, 2, depth])
    hr_in = din("hnorm_r", [64, 2, depth])
    cst_in = din("consts", [128, 128 + 64 + 1])
    y_out = nc.dram_tensor("yT", [D, S], F32, kind="ExternalOutput").ap()

    wb_in = [dscr("wb_in%d" % l, [D, IN_DIM], BF16, False) for l in range(depth)]
    wb_uq = [dscr("wb_uq%d" % l, [512, 3072], BF16, False) for l in range(depth)]
    wb_uk = [dscr("wb_uk%d" % l, [512, NH, 128], BF16, False) for l in range(depth)]
    wb_uv = [dscr("wb_uv%d" % l, [512, NH, 128], BF16, False) for l in range(depth)]
    wb_pc = [dscr("wb_pc%d" % l, [D, D], BF16, False) for l in range(depth)]
    wb_pm = [dscr("wb_pm%d" % l, [D, D], BF16, False) for l in range(depth)]
    wb_o = [dscr("wb_o%d" % l, [D, D], BF16, False) for l in range(depth)]
    cosT = dscr("cosT", [64, S], F32)
    sinT = dscr("sinT", [64, S], F32)
    cbrT = dscr("cbrT", [D, S], BF16)
    smgT = dscr("smgT", [D, S], BF16)
    tgcT = dscr("tgcT", [D, S], BF16)
    tgmT = dscr("tgmT", [D, S], BF16)
    qTn = dscr("qTn", [NH, 128, S], BF16)
    qTr = dscr("qTr", [NH, 64, S], BF16)
    kTn = dscr("kTn", [NH, 128, S], BF16)
    kTr = dscr("kTr", [64, S], BF16)
    Vd = dscr("Vd", [S, D], BF16)
    Fkd = dscr("Fkd", [S, NH], F32)
    attT = dscr("attT", [D, S], BF16)
    x1T = dscr("x1T", [D, S], F32)

    with ExitStack() as stack:
        sch = Sched(nc, stack)

        uid = {"i": 0}

        def uname(name):
            uid["i"] += 1
            return "s%d_%s" % (uid["i"], name)

        def sb(st, name, shape, dt, keep=False):
            b = Buf(st.enter_context(nc.sbuf_tensor(uname(name), list(shape), dt)))
            if keep:
                b._keep = True
            return b

        class KBuf(Buf):
            __slots__ = ("_keep",)

        def sbk(st, name, shape, dt):
            b = KBuf(st.enter_context(nc.sbuf_tensor(uname(name), list(shape), dt)))
            b._keep = False
            return b

        def dbuf():
            b = KBuf(None, acc=True)
            b._keep = False
            return b

        dr = {n: dbuf() for n in ("cos", "cbr", "smg", "tgc", "tgm", "qn", "qr", "kn", "kr", "V", "Fk", "att", "x1", "y")}
        dw = {}
        for l in range(depth):
            for n in ("ina", "inb", "uq", "uk", "uv", "pc", "pm", "o"):
                b = dbuf()
                b._keep = True
                dw[(n, l)] = b
        d_x0 = dbuf()

        ps = [Buf(stack.enter_context(nc.psum_tensor("ps%d" % i, [128, 512], F32)), excl=True) for i in range(8)]
        rot = {"i": 0}

        def psg(n=5):
            b = ps[rot["i"] % n]
            rot["i"] += 1
            return b

        ident_f = sbk(stack, "ident_f", [128, 193], F32)
        ident_b = sbk(stack, "ident_b", [128, 128], BF16)
        Rm_b = sbk(stack, "Rm_b", [64, 64], BF16)
        ones_b = sbk(stack, "ones_b", [128, 128], BF16)
        ones_d = sbk(stack, "ones_d", [128, 128], BF16)
        ones_q = sbk(stack, "ones_q", [128, 128], BF16)
        ones_f = sbk(stack, "ones_f", [128, 128], F32)
        negpi = sbk(stack, "negpi", [128, 1], F32)
        epsb = sbk(stack, "epsb", [128, 2], F32)
        pvec = sbk(stack, "pvec", [128, 5, depth, NCH], F32)
        lnb_h = sbk(stack, "lnb_h", [128, depth, NCH], F32)
        cwT = sbk(stack, "cwT", [128, depth, NCH, CW], F32)
        lorag = sbk(stack, "lorag", [128, 2, depth, 4], F32)
        hn_n = sbk(stack, "hn_n", [128, 2, depth], F32)
        hn_r = sbk(stack, "hn_r", [64, 2, depth], F32)

        for dst, src in ((ident_f, cst_in), (pvec, pv_in), (cwT, cw_in), (lorag, lg_in), (hn_n, hn_in), (hn_r, hr_in)):
            sch.dma("sp", dst[:], src[:], dst, writes=[dst])
        sch.op("dve", lambda e: e.tensor_copy(ident_b[:], ident_f[:, 0:128]), [ident_f], [ident_b])
        sch.op("dve", lambda e: e.tensor_copy(Rm_b[:], ident_f[0:64, 128:192]), [ident_f], [Rm_b])
        sch.op("pool", lambda e: e.memset(ones_b[:], 1.0), [], [ones_b])
        sch.op("pool", lambda e: e.memset(ones_d[:], 1.0 / 2048.0), [], [ones_d])
        sch.op("pool", lambda e: e.memset(ones_q[:], 1.0 / 512.0), [], [ones_q])
        sch.op("pool", lambda e: e.memset(ones_f[:], 1.0), [], [ones_f])
        sch.op("pool", lambda e: e.memset(negpi[:], -PI), [], [negpi])
        sch.op("pool", lambda e: e.memset(epsb[:, 0:1], EPS), [], [epsb])
        sch.op("pool", lambda e: e.memset(epsb[:, 1:2], 192.0 * EPS), [], [epsb])

        def rsq(out_ap, out_buf, in_ap, in_buf, tmp_ap, tmp_buf, power=-0.5, eps_col=0, rows=128):
            sch.op("act", lambda e: e.activation(out=tmp_ap, in_=in_ap, func=AF.Ln, bias=epsb[0:rows, eps_col:eps_col + 1], scale=1.0),
                   [in_buf, epsb], [tmp_buf])
            sch.op("act", lambda e: e.activation(out=out_ap, in_=tmp_ap, func=AF.Exp, scale=power), [tmp_buf], [out_buf])


        cast_jobs = []

        def cast(dst, src, key, rows, cols, rb=512, cb=2048, cbeg=0):
            for r0 in range(0, rows, rb):
                for c0 in range(cbeg, cols, cb):
                    c1 = min(cols, c0 + cb)
                    cast_jobs.append(lambda r0=r0, c0=c0, c1=c1: sch.dma(
                        "pool", dst[r0:r0 + rb, c0:c1], src[r0:r0 + rb, c0:c1], dw[key], writes=[dw[key]]))

        def cast_kv(l):
            kv4 = w_ukv[l].rearrange("k (h two d) -> k h two d", two=2, d=128)
            for r0 in range(0, 512, 128):
                cast_jobs.append(lambda r0=r0: sch.dma("pool", wb_uk[l][r0:r0 + 128], kv4[r0:r0 + 128, :, 0, :], dw[("uk", l)],
                                                       writes=[dw[("uk", l)]]))
                cast_jobs.append(lambda r0=r0: sch.dma("pool", wb_uv[l][r0:r0 + 128], kv4[r0:r0 + 128, :, 1, :], dw[("uv", l)],
                                                       writes=[dw[("uv", l)]]))

        n_upfront = None
        for l in range(depth):
            cast(wb_in[l], w_in[l], ("ina", l), D, OFF_QD)
            cast(wb_in[l], w_in[l], ("inb", l), D, IN_DIM, cbeg=OFF_QD)
            cast(wb_uq[l], w_uq[l], ("uq", l), 512, 3072)
            cast_kv(l)
            if n_upfront is None:
                n_upfront = len(cast_jobs)
            cast(wb_pc[l], w_pc[l], ("pc", l), D, D)
            cast(wb_pm[l], w_pm[l], ("pm", l), D, D)
            cast(wb_o[l], w_o[l], ("o", l), D, D)

        def emit_casts(n):
            for _ in range(min(n, len(cast_jobs))):
                cast_jobs.pop(0)()

        with ExitStack() as st:
            posi = sb(st, "posi", [64, S], I32)
            ang = sb(st, "ang", [64, S], F32)
            tmp = sb(st, "tmpT", [64, S], F32)
            tab = sb(st, "tabT", [64, S], F32)
            pos_b = pos_in.to_broadcast([64, S])
            sch.dma("sp", posi[:], pos_b, posi, writes=[posi])
            sch.op("dve", lambda e: e.tensor_copy(ang[:], posi[:]), [posi], [ang])
            sch.op("dve", lambda e: e.tensor_scalar(ang[:], ang[:], ident_f[0:64, 192:193], None, ALU.mult), [ang, ident_f], [ang])
            for shift, dstT in ((0.0, sinT), (0.5 * PI, cosT)):
                sch.op("dve", lambda e: e.tensor_scalar(tmp[:], ang[:], shift, 1.0 / (2.0 * PI), ALU.add, ALU.mult), [ang], [tmp])
                sch.op("dve", lambda e: e.tensor_copy(posi[:], tmp[:]), [tmp], [posi])
                sch.op("dve", lambda e: e.tensor_copy(tmp[:], posi[:]), [posi], [tmp])
                sch.op("dve", lambda e: e.scalar_tensor_tensor(out=tmp[:], in0=tmp[:], scalar=-2.0 * PI, in1=ang[:],
                                                               op0=ALU.mult, op1=ALU.add), [tmp, ang], [tmp])
                sch.op("dve", lambda e: e.tensor_scalar(tmp[:], tmp[:], shift, None, ALU.add), [tmp], [tmp])
                sch.op("dve", lambda e: e.tensor_scalar(tmp[:], tmp[:], -PI, PI, ALU.max, ALU.min), [tmp], [tmp])
                sch.op("act", lambda e: e.activation(out=tab[:], in_=tmp[:], func=AF.Sin), [tmp], [tab])
                sch.dma("pool", dstT[:], tab[:], tab, reads=[tab], writes=[dr["cos"]])
            sch.barrier(skip_keep=True)
        emit_casts(n_upfront)

        def wview(wb, c0, c1):
            return wb.rearrange("(k p) n -> p k n", p=128)[:, :, c0:c1]

        def mm16(pbuf, wblk, coff, m, rhs_buf, rhs_fn, extra_reads=(), nk=NCH, mrows=128):
            def fn(e):
                ins = None
                for k in range(nk):
                    ins = e.matmul(pbuf[0:m, :] if m < 128 else pbuf[:], wblk[:, k, coff:coff + m], rhs_fn(k),
                                   start=(k == 0), stop=(k == nk - 1))
                return ins
            return sch.op("pe", fn, [wblk, rhs_buf] + list(extra_reads), [pbuf])

        def phase_A(l, xsrc, xsrc_buf):
            if l > 0:
                emit_casts(len(cast_jobs))
            g_norm = lambda c: pvec[:, 0, l, c:c + 1]
            g_cb = lambda c: pvec[:, 1, l, c:c + 1]
            g_lg = lambda c: pvec[:, 2, l, c:c + 1]
            g_lb = lambda c: pvec[:, 3, l, c:c + 1]
            ntile = NTILE if ntile_limit is None else ntile_limit
            with ExitStack() as st:
                xt = [sb(st, "xt%d" % i, [128, TT], F32) for i in range(3)]
                xr = {"i": 0}
                hT = [sb(st, "hT%d" % i, [128, NCH, TT], BF16) for i in range(2)]
                sq = [sb(st, "sq%d" % i, [128, TT], BF16) for i in range(2)]
                rstd = sb(st, "rstd", [128, TT], F32)
                lnt = sb(st, "lnt", [128, TT], F32)
                wblk = [sb(st, "wblk%d" % i, [128, NCH, 256], BF16) for i in range(4)]
                wr = {"i": 0}
                vst = sb(st, "vst", [128, NCH, TT], BF16)
                uall = sb(st, "uall", [128, NCH, TT + 32], BF16)
                vstb = [Buf(vst.t) for _ in range(NCH)]
                uallb = [Buf(uall.t) for _ in range(NCH)]
                dg = [sb(st, "dg%d" % i, [128, CW, 128], BF16) for i in range(2)]
                tg = [sb(st, "tg%d" % i, [128, TT], F32) for i in range(2)]
                v2 = [sb(st, "v2%d" % i, [128, TT], BF16) for i in range(2)]
                lnA = sb(st, "lnA", [128, TT], F32)
                lnB = sb(st, "lnB", [128, TT], F32)
                lnm = sb(st, "lnm", [128, TT], F32)
                sgt = [sb(st, "sgt%d" % i, [128, TT], F32) for i in range(1)]
                zt = [sb(st, "zt%d" % i, [128, TT], F32) for i in range(1)]
                szt = [sb(st, "szt%d" % i, [128, TT], F32) for i in range(1)]
                stg = [sb(st, "stg%d" % i, [128, TT], BF16) for i in range(4)]
                sr = {"i": 0}
                qd = sb(st, "qd", [128, 4, TT], BF16)
                qsq = sb(st, "qsq", [128, 4, TT], BF16)
                kvd = sb(st, "kvd", [128, 4, TT], BF16)
                kvsq = sb(st, "kvsq", [128, 4, TT], BF16)
                Rq = sb(st, "Rq", [128, TT], F32)
                Rq2 = sb(st, "Rq2", [128, TT], F32)
                Sk = sb(st, "Sk", [64, TT], F32)
                rkv_tm = sb(st, "rkv_tm", [128, 4], F32)
                r2kv_tm = sb(st, "r2kv_tm", [128, 4], F32)
                sspe_tm = sb(st, "sspe_tm", [128, 4], F32)
                fk_tm = sb(st, "fk_tm", [128, 4, NH], F32)
                cs_t = sb(st, "cs_t", [64, 2, TT], F32)
                kpsq = sb(st, "kpsq", [64, TT], BF16)
                rf = [sb(st, "rf%d" % i, [64, TT], F32) for i in range(2)]
                rb_ = [sb(st, "rb%d" % i, [64, TT], BF16) for i in range(2)]
                r1 = [sb(st, "r1%d" % i, [64, TT], F32) for i in range(2)]
                r2 = [sb(st, "r2%d" % i, [64, TT], F32) for i in range(2)]
                Ft = [sb(st, "Ft%d" % i, [128, TT], F32) for i in range(2)]
                qsn = [sb(st, "qsn%d" % i, [128, TT], BF16) for i in range(2)]
                qsr = [sb(st, "qsr%d" % i, [64, TT], BF16) for i in range(2)]
                ksq = [sb(st, "ksq%d" % i, [128, TT], BF16) for i in range(2)]
                stg64 = [sb(st, "stg64_%d" % i, [64, TT], BF16) for i in range(2)]
                s64 = {"i": 0}

                def next_stage():
                    b = stg[sr["i"] % len(stg)]
                    sr["i"] += 1
                    return b

                def next_stage64():
                    b = stg64[s64["i"] % len(stg64)]
                    s64["i"] += 1
                    return b

                def load_w(wb, key, c0, ncols, nk=NCH):
                    if key is None:
                        key = ("ina", l) if c0 < OFF_QD else ("inb", l)
                    b = wblk[wr["i"] % len(wblk)]
                    wr["i"] += 1
                    sch.dma("sp", b[:, 0:nk, 0:ncols], wview(wb, c0, c0 + ncols), b, reads=[dw[key]], writes=[b])
                    return b

                sch.op("pool", lambda e: e.memset(uall[:], 0.0), [], uallb)
                pS1, pS2, pT = ps[5], ps[6], ps[7]

                def a0_steps(jn):
                    tsn = slice(jn * TT, (jn + 1) * TT)
                    hn = hT[jn % 2]
                    pss = ps[7]

                    def load_x(c):
                        x_t = xt[xr["i"] % len(xt)]
                        xr["i"] += 1
                        sch.dma("sp", x_t[:], xsrc[c * 128:(c + 1) * 128, tsn], x_t, reads=[xsrc_buf], writes=[x_t])
                        return x_t
                    steps = []

                    def st1(c):
                        if c < NCH:
                            x_t = load_x(c)
                            s_ = sq[c % 2]
                            sch.op("act", lambda e: e.activation(out=s_[:], in_=x_t[:], func=AF.Square), [x_t], [s_])
                        if c >= 1:
                            p_ = sq[(c - 1) % 2]
                            sch.op("pe", lambda e: e.matmul(pss[:], ones_d[:], p_[:], start=(c - 1 == 0), stop=(c - 1 == NCH - 1)),
                                   [ones_d, p_], [pss])

                    def st2():
                        rsq(rstd[:], rstd, pss[:], pss, lnt[:], lnt)

                    def st3(c):
                        x_t = load_x(c)
                        sch.op("dve", lambda e: e.scalar_tensor_tensor(out=hn[:, c, :], in0=x_t[:], scalar=g_norm(c),
                                                                       in1=rstd[:], op0=ALU.mult, op1=ALU.mult),
                               [x_t, rstd, pvec], [hn])
                    for c in range(NCH + 1):
                        steps.append(lambda c=c: st1(c))
                    steps.append(st2)
                    for c in range(NCH):
                        steps.append(lambda c=c: st3(c))
                    return steps

                for f_ in a0_steps(0):
                    f_()

                for j in range(ntile):
                    tsl = slice(j * TT, (j + 1) * TT)
                    h = hT[j % 2]
                    emit_casts(2)
                    a1w = {}

                    def a1_s1(m):
                        mb, mi = m // 2, m % 2
                        if mi == 0:
                            a1w["wa"] = load_w(wb_in[l], None, OFF_A + mb * 256, 256)
                            a1w["wg"] = load_w(wb_in[l], None, OFF_GLU + mb * 256, 256)
                        wa, wg = a1w["wa"], a1w["wg"]
                        pA, pG = ps[(m % 2) * 2], ps[(m % 2) * 2 + 1]
                        mm16(pA, wa, mi * 128, 128, h, lambda k: h[:, k, :])
                        mm16(pG, wg, mi * 128, 128, h, lambda k: h[:, k, :])
                        t_ = tg[m % 2]
                        sch.op("act", lambda e: e.activation(out=t_[:], in_=pG[:], func=AF.Tanh, scale=0.5), [pG], [t_])
                        if j > 0:
                            sch.op("pool", lambda e: e.tensor_copy(uall[:, m, 0:30], uall[:, m, TT:TT + 30]), [uallb[m]], [uallb[m]])
                        sch.op("dve", lambda e: e.scalar_tensor_tensor(out=uall[:, m, 30:30 + TT], in0=t_[:], scalar=1.0,
                                                                       in1=pA[:], op0=ALU.add, op1=ALU.mult),
                               [t_, pA], [uallb[m]])
                        d_ = dg[m % 2]
                        in0 = ident_b[:, None, :].to_broadcast([128, CW, 128])
                        in1 = cwT[:, l, m, :, None].to_broadcast([128, CW, 128])
                        sch.op("pool", lambda e: e.tensor_tensor(d_[:], in0, in1, ALU.mult), [ident_b, cwT], [d_])

                    def a1_s2(m):
                        d_ = dg[m % 2]
                        pC = ps[4] if m % 2 == 0 else ps[7]

                        def convmm(e):
                            ins = None
                            for k in range(CW):
                                ins = e.matmul(pC[:], d_[:, k, :], uall[:, m, k:k + TT], start=(k == 0), stop=(k == CW - 1))
                            return ins
                        sch.op("pe", convmm, [d_, uallb[m]], [pC])
                        sch.op("act", lambda e: e.activation(out=vst[:, m, :], in_=pC[:], func=AF.Identity,
                                                             bias=g_cb(m), scale=0.5), [pC, pvec], [vstb[m]])
                        q_ = v2[m % 2]
                        sch.op("act", lambda e: e.activation(out=q_[:], in_=vst[:, m, :], func=AF.Square), [vstb[m]], [q_])

                    def a1_s3(m):
                        q_ = v2[m % 2]
                        sch.op("pe", lambda e: e.matmul(pS1[:], ones_d[:], vst[:, m, :], start=(m == 0), stop=(m == NCH - 1)),
                               [ones_d, vstb[m]], [pS1])
                        sch.op("pe", lambda e: e.matmul(pS2[:], ones_d[:], q_[:], start=(m == 0), stop=(m == NCH - 1)),
                               [ones_d, q_], [pS2])

                    if "A1" in A_PARTS:
                        for i in range(NCH + 2):
                            if i < NCH:
                                a1_s1(i)
                            if 0 <= i - 1 < NCH:
                                a1_s2(i - 1)
                            if 0 <= i - 2 < NCH:
                                a1_s3(i - 2)
                    if "A1" not in A_PARTS:
                        sch.op("pe", lambda e: e.matmul(pS1[:], ones_d[:], ones_d[:, 0:1].to_broadcast([128, 512]) if False else sq[0][:], start=True, stop=True), [ones_d, sq[0]], [pS1])
                        sch.op("pe", lambda e: e.matmul(pS2[:], ones_d[:], sq[0][:], start=True, stop=True), [ones_d, sq[0]], [pS2])
                    sch.op("act", lambda e: e.activation(out=lnm[:], in_=pS1[:], func=AF.Identity), [pS1], [lnm])
                    sch.op("dve", lambda e: e.tensor_tensor(lnB[:], lnm[:], lnm[:], ALU.mult), [lnm], [lnB])
                    sch.op("dve", lambda e: e.tensor_tensor(lnA[:], pS2[:], lnB[:], ALU.subtract), [pS2, lnB], [lnA])
                    sch.op("dve", lambda e: e.tensor_scalar(lnA[:], lnA[:], 0.0, None, ALU.max), [lnA], [lnA])
                    rsq(lnA[:], lnA, lnA[:], lnA, lnt[:], lnt)
                    sch.op("dve", lambda e: e.scalar_tensor_tensor(out=lnB[:], in0=lnm[:], scalar=-1.0, in1=lnA[:],
                                                                   op0=ALU.mult, op1=ALU.mult), [lnm, lnA], [lnB])
                    for mb in (range(8) if "A1" in A_PARTS else ()):
                        wc = load_w(wb_in[l], None, OFF_CG + mb * 256, 256)
                        for mi in range(2):
                            m = mb * 2 + mi
                            pA = psg()
                            mm16(pA, wc, mi * 128, 128, h, lambda k: h[:, k, :])
                            s_ = sgt[0]
                            sch.op("act", lambda e: e.activation(out=s_[:], in_=pA[:], func=AF.Silu), [pA], [s_])
                            z_ = zt[0]
                            sch.op("dve", lambda e: e.tensor_tensor(z_[:], vst[:, m, :], lnA[:], ALU.mult), [vstb[m], lnA], [z_])
                            sch.op("dve", lambda e: e.tensor_tensor(z_[:], z_[:], lnB[:], ALU.add), [z_, lnB], [z_])
                            y_ = szt[0]
                            sch.op("act", lambda e: e.activation(out=y_[:], in_=z_[:], func=AF.Silu, bias=g_lb(m), scale=g_lg(m)),
                                   [z_, pvec], [y_])
                            o_ = next_stage()
                            sch.op("dve", lambda e: e.tensor_tensor(o_[:], y_[:], s_[:], ALU.mult), [y_, s_], [o_])
                            sch.dma("pool", cbrT[m * 128:(m + 1) * 128, tsl], o_[:], o_, reads=[o_], writes=[dr["cbr"]])

                    gate_tasks = []
                    if "A2" in A_PARTS:
                        for off, func, scale, dst, key in ((OFF_MG, AF.Silu, 1.0, smgT, "smg"), (OFF_GC, AF.Tanh, 0.5, tgcT, "tgc"),
                                                           (OFF_GM, AF.Tanh, 0.5, tgmT, "tgm")):
                            for m in range(NCH):
                                gate_tasks.append((off, func, scale, dst, key, m))
                    gw = {}
                    gcnt = {"i": 0}

                    def run_gate(fixed_banks=True):
                        if not gate_tasks:
                            return
                        off, func, scale, dst, key, m = gate_tasks.pop(0)
                        mb, mi = m // 2, m % 2
                        if mi == 0:
                            gw["w"] = load_w(wb_in[l], None, off + mb * 256, 256)
                        wc = gw["w"]
                        if fixed_banks:
                            pA = ps[5 + gcnt["i"] % 2]
                            gcnt["i"] += 1
                        else:
                            pA = psg()
                        mm16(pA, wc, mi * 128, 128, h, lambda k: h[:, k, :])
                        o_ = next_stage()
                        sch.op("act", lambda e: e.activation(out=o_[:], in_=pA[:], func=func, scale=scale), [pA], [o_])
                        sch.dma("pool", dst[m * 128:(m + 1) * 128, tsl], o_[:], o_, reads=[o_], writes=[dr[key]])

                    def finish_tile():
                        nxt = a0_steps(j + 1) if j + 1 < ntile else []
                        while gate_tasks or nxt:
                            run_gate(fixed_banks=False)
                            for _ in range(3):
                                if nxt:
                                    nxt.pop(0)()

                    if "A3" not in A_PARTS:
                        finish_tile()
                        continue
                    sch.dma("sp", cs_t[:, 0, :], cosT[:, tsl], cs_t, reads=[dr["cos"]], writes=[cs_t])
                    sch.dma("sp", cs_t[:, 1, :], sinT[:, tsl], cs_t, reads=[dr["cos"]], writes=[cs_t])
                    for (off, dstd, dsq, gi) in ((OFF_QD, qd, qsq, 0), (OFF_KVD, kvd, kvsq, 1)):
                        for c in range(4):
                            if c % 2 == 0:
                                wc = load_w(wb_in[l], None, off + c * 128, 256)
                            pA = psg()
                            mm16(pA, wc, (c % 2) * 128, 128, h, lambda k: h[:, k, :])
                            sch.op("act", lambda e: e.activation(out=dsq[:, c, :], in_=pA[:], func=AF.Square), [pA], [dsq])
                            sch.op("dve", lambda e: e.tensor_scalar(dstd[:, c, :], pA[:], lorag[:, gi, l, c:c + 1], None, ALU.mult),
                                   [pA, lorag], [dstd])
                    pq = psg()
                    sch.op("pe", lambda e: [e.matmul(pq[:], ones_q[:], qsq[:, c, :], start=(c == 0), stop=(c == 3)) for c in range(4)][-1],
                           [ones_q, qsq], [pq])
                    rsq(Rq[:], Rq, pq[:], pq, lnt[:], lnt)
                    sch.op("dve", lambda e: e.scalar_tensor_tensor(out=Rq2[:], in0=Rq[:], scalar=1.0 / 192.0, in1=Rq[:],
                                                                   op0=ALU.mult, op1=ALU.mult), [Rq], [Rq2])
                    pk = psg()
                    sch.op("pe", lambda e: [e.matmul(pk[:], ones_q[:], kvsq[:, c, :], start=(c == 0), stop=(c == 3)) for c in range(4)][-1],
                           [ones_q, kvsq], [pk])
                    rsq(Sk[:], Sk, pk[0:64, :], pk, lnt[0:64, :], lnt, power=0.5, rows=64)
                    def kvstat(e):
                        ins = None
                        for tb in range(4):
                            for c in range(4):
                                ins = e.matmul(pT[:, tb:tb + 1], kvsq[:, c, tb * 128:(tb + 1) * 128], ones_q[:, 0:1],
                                               start=(c == 0), stop=(c == 3))
                        return ins
                    sch.op("pe", kvstat, [kvsq, ones_q], [pT])
                    rsq(rkv_tm[:], rkv_tm, pT[:, 0:4], pT, lnt[:, 0:4], lnt)
                    sch.op("dve", lambda e: e.tensor_tensor(r2kv_tm[:], rkv_tm[:], rkv_tm[:], ALU.mult), [rkv_tm], [r2kv_tm])

                    if A3_STOP <= 1:
                        finish_tile()
                        continue
                    wc = load_w(wb_in[l], None, OFF_KPE, 64)
                    pK = psg()
                    mm16(pK, wc, 0, 64, h, lambda k: h[:, k, :])
                    sch.op("act", lambda e: e.activation(out=kpsq[:], in_=pK[0:64, :], func=AF.Square), [pK], [kpsq])

                    def pestat(e):
                        ins = None
                        for tb in range(4):
                            ins = e.matmul(pT[:, 4 + tb:5 + tb], kpsq[:, tb * 128:(tb + 1) * 128], ones_b[0:64, 0:1],
                                           start=True, stop=True)
                        return ins
                    sch.op("pe", pestat, [kpsq, ones_b], [pT])
                    sch.op("dve", lambda e: e.tensor_copy(sspe_tm[:], pT[:, 4:8]), [pT], [sspe_tm])

                    def rope(src_f, src_b, out_ap, out_buf, extra_mul=None, extra_buf=None, ri=0, pR=None):
                        pR = psg() if pR is None else pR
                        sch.op("pe", lambda e: e.matmul(pR[0:64, :], Rm_b[:], src_b[:], start=True, stop=True), [Rm_b, src_b], [pR])
                        a_, b_ = r1[ri], r2[ri]
                        sch.op("pool", lambda e: e.tensor_tensor(a_[:], src_f[:], cs_t[:, 0, :], ALU.mult), [src_f, cs_t], [a_])
                        sch.op("dve", lambda e: e.tensor_tensor(b_[:], pR[0:64, :], cs_t[:, 1, :], ALU.mult), [pR, cs_t], [b_])
                        if extra_mul is None:
                            sch.op("pool", lambda e: e.tensor_tensor(out_ap, a_[:], b_[:], ALU.add), [a_, b_], [out_buf])
                        else:
                            sch.op("pool", lambda e: e.tensor_tensor(a_[:], a_[:], b_[:], ALU.add), [a_, b_], [a_])
                            sch.op("pool", lambda e: e.tensor_tensor(out_ap, a_[:], extra_mul, ALU.mult), [a_, extra_buf], [out_buf])

                    sch.op("dve", lambda e: e.tensor_scalar(rf[0][:], pK[0:64, :], hn_r[:, 1, l:l + 1], None, ALU.mult), [pK, hn_r], [rf[0]])
                    sch.op("pool", lambda e: e.tensor_copy(rb_[0][:], rf[0][:]), [rf[0]], [rb_[0]])
                    o64 = next_stage64()
                    rope(rf[0], rb_[0], o64[:], o64, extra_mul=Sk[:], extra_buf=Sk, ri=0)
                    sch.dma("pool", kTr[:, tsl], o64[:], o64, reads=[o64], writes=[dr["kr"]])

                    if A3_STOP <= 2:
                        finish_tile()
                        continue
                    qw = {}

                    def q_s1(hh):
                        hg, hi = hh // 4, hh % 4
                        if hi == 0:
                            b = wblk[wr["i"] % len(wblk)]
                            wr["i"] += 1
                            wq3 = b[:].rearrange("p k n -> p (k n)")[:, 0:4 * 768].rearrange("p (k n) -> p k n", k=4)
                            sch.dma("sp", wq3, wb_uq[l].rearrange("(k p) n -> p k n", p=128)[:, :, hg * 768:(hg + 1) * 768], b,
                                    reads=[dw[("uq", l)]], writes=[b])
                            qw["b"], qw["w"] = b, wq3
                        b, wq3 = qw["b"], qw["w"]
                        ri = hh % 2
                        pQn, pQr = ps[ri * 2], ps[ri * 2 + 1]
                        sch.op("pe", lambda e: [e.matmul(pQn[:], wq3[:, c, hi * 192:hi * 192 + 128], qd[:, c, :],
                                                         start=(c == 0), stop=(c == 3)) for c in range(4)][-1], [b, qd], [pQn])
                        sch.op("pe", lambda e: [e.matmul(pQr[0:64, :], wq3[:, c, hi * 192 + 128:hi * 192 + 192], qd[:, c, :],
                                                         start=(c == 0), stop=(c == 3)) for c in range(4)][-1], [b, qd], [pQr])
                        sn, sr_ = qsn[ri], qsr[ri]
                        sch.op("act", lambda e: e.activation(out=sn[:], in_=pQn[:], func=AF.Square), [pQn], [sn])
                        sch.op("act", lambda e: e.activation(out=sr_[:], in_=pQr[0:64, :], func=AF.Square), [pQr], [sr_])

                    def q_s2(hh):
                        ri = hh % 2
                        pQn, pQr = ps[ri * 2], ps[ri * 2 + 1]
                        sn, sr_ = qsn[ri], qsr[ri]
                        pSS = ps[4]

                        def ssmm(e):
                            e.matmul(pSS[:], ones_b[:], sn[:], start=True, stop=False)
                            return e.matmul(pSS[:], ones_b[0:64, :], sr_[:], start=False, stop=True)
                        sch.op("pe", ssmm, [ones_b, sn, sr_], [pSS])
                        F_ = Ft[ri]
                        sch.op("dve", lambda e: e.tensor_tensor(F_[:], pSS[:], Rq2[:], ALU.mult), [pSS, Rq2], [F_])
                        rsq(F_[:], F_, F_[:], F_, lnt[:], lnt)
                        sch.op("dve", lambda e: e.tensor_tensor(F_[:], F_[:], Rq[:], ALU.mult), [F_, Rq], [F_])
                        o_ = next_stage()
                        sch.op("dve", lambda e: e.scalar_tensor_tensor(out=o_[:], in0=pQn[:], scalar=hn_n[:, 0, l:l + 1], in1=F_[:],
                                                                       op0=ALU.mult, op1=ALU.mult), [pQn, hn_n, F_], [o_])
                        sch.dma("pool", qTn[hh, :, tsl], o_[:], o_, reads=[o_], writes=[dr["qn"]])
                        sch.op("dve", lambda e: e.scalar_tensor_tensor(out=rf[ri][:], in0=pQr[0:64, :], scalar=hn_r[:, 0, l:l + 1],
                                                                       in1=F_[0:64, :], op0=ALU.mult, op1=ALU.mult),
                               [pQr, hn_r, F_], [rf[ri]])
                        sch.op("act", lambda e: e.activation(out=rb_[ri][:], in_=rf[ri][:], func=AF.Identity), [rf[ri]], [rb_[ri]])

                    def q_s3(hh):
                        ri = hh % 2
                        o64 = next_stage64()
                        rope(rf[ri], rb_[ri], o64[:], o64, ri=ri, pR=ps[7])
                        sch.dma("pool", qTr[hh, :, tsl], o64[:], o64, reads=[o64], writes=[dr["qr"]])

                    for i in range(NH + 2):
                        if i < NH:
                            q_s1(i)
                        if 0 <= i - 1 < NH:
                            q_s2(i - 1)
                        if 0 <= i - 2 < NH:
                            q_s3(i - 2)
                        run_gate()

                    if A3_STOP <= 3:
                        finish_tile()
                        continue
                    pF = ps[7]
                    kw = {}

                    def k_s1(hh):
                        if hh % 4 == 0:
                            bk = wblk[wr["i"] % len(wblk)]
                            wr["i"] += 1
                            wk3 = bk[:].rearrange("p k n -> p (k n)")[:, 0:4 * 512].rearrange("p (k n) -> p k n", k=4)
                            sch.dma("sp", wk3, wb_uk[l].rearrange("(k p) h d -> p k (h d)", p=128)[:, :, (hh // 4) * 512:(hh // 4 + 1) * 512],
                                    bk, reads=[dw[("uk", l)]], writes=[bk])
                            kw["b"], kw["w"] = bk, wk3
                        bk, wk3 = kw["b"], kw["w"]
                        pKn = psg()
                        sch.op("pe", lambda e: [e.matmul(pKn[:], wk3[:, c, (hh % 4) * 128:(hh % 4 + 1) * 128], kvd[:, c, :],
                                                         start=(c == 0), stop=(c == 3)) for c in range(4)][-1], [bk, kvd], [pKn])
                        s_ = ksq[hh % 2]
                        sch.op("act", lambda e: e.activation(out=s_[:], in_=pKn[:], func=AF.Square), [pKn], [s_])
                        o_ = next_stage()
                        sch.op("dve", lambda e: e.tensor_scalar(o_[:], pKn[:], hn_n[:, 1, l:l + 1], None, ALU.mult), [pKn, hn_n], [o_])
                        sch.dma("pool", kTn[hh, :, tsl], o_[:], o_, reads=[o_], writes=[dr["kn"]])

                    def k_s2(hh):
                        s_ = ksq[hh % 2]

                        def kstat(e):
                            ins = None
                            for tb in range(4):
                                ins = e.matmul(pF[:, 8 + tb * NH + hh:9 + tb * NH + hh], s_[:, tb * 128:(tb + 1) * 128], ones_b[:, 0:1],
                                               start=True, stop=True)
                            return ins
                        sch.op("pe", kstat, [s_, ones_b], [pF])

                    for i in range(NH + 1):
                        if i < NH:
                            k_s1(i)
                        if 0 <= i - 1 < NH:
                            k_s2(i - 1)
                        if i < NH:
                            run_gate()
                    ssn = pF[:, 8:8 + 4 * NH].rearrange("p (t h) -> p t h", t=4)
                    sch.op("dve", lambda e: e.tensor_tensor(fk_tm[:], ssn, r2kv_tm[:, :, None].to_broadcast([128, 4, NH]), ALU.mult),
                           [pF, r2kv_tm], [fk_tm])
                    sch.op("dve", lambda e: e.tensor_tensor(fk_tm[:], fk_tm[:], sspe_tm[:, :, None].to_broadcast([128, 4, NH]), ALU.add),
                           [fk_tm, sspe_tm], [fk_tm])
                    rsq(fk_tm[:], fk_tm, fk_tm[:], fk_tm, lnt[:, 0:4 * NH].rearrange("p (t h) -> p t h", t=4), lnt, eps_col=1)
                    sch.op("dve", lambda e: e.tensor_tensor(fk_tm[:], fk_tm[:], rkv_tm[:, :, None].to_broadcast([128, 4, NH]), ALU.mult),
                           [fk_tm, rkv_tm], [fk_tm])
                    sch.dma("pool", Fkd[tsl, :].rearrange("(t p) h -> p t h", p=128), fk_tm[:], fk_tm, reads=[fk_tm], writes=[dr["Fk"]])

                    if A3_STOP <= 4:
                        finish_tile()
                        continue
                    for hg in range(4):
                        bv = wblk[wr["i"] % len(wblk)]
                        wr["i"] += 1
                        wv3 = bv[:].rearrange("p k n -> p (k n)")[:, 0:4 * 512].rearrange("p (k n) -> p k n", k=4)
                        sch.dma("sp", wv3, wb_uv[l].rearrange("(k p) h d -> p k (h d)", p=128)[:, :, hg * 512:(hg + 1) * 512], bv,
                                reads=[dw[("uv", l)]], writes=[bv])
                        for tb in range(4):
                            pV = psg()
                            sch.op("pe", lambda e: [e.matmul(pV[:], kvd[:, c, tb * 128:(tb + 1) * 128], wv3[:, c, :],
                                                             start=(c == 0), stop=(c == 3)) for c in range(4)][-1], [bv, kvd], [pV])
                            o_ = next_stage()
                            sch.op("dve", lambda e: e.tensor_scalar(o_[:], pV[:], rkv_tm[:, tb:tb + 1], None, ALU.mult),
                                   [pV, rkv_tm], [o_])
                            sch.dma("pool", Vd[j * TT + tb * 128:j * TT + (tb + 1) * 128, hg * 512:(hg + 1) * 512], o_[:], o_,
                                    reads=[o_], writes=[dr["V"]])
                    finish_tile()
                sch.barrier()

        def phase_B(l):
            ntile = NTILE if ntile_limit is None else ntile_limit
            with ExitStack() as st:
                krope = sb(st, "krope", [128, S], BF16)
                fk_all = sb(st, "fk_all", [128, S // 128, NH], F32)
                kn = [sb(st, "kn%d" % i, [128, S], BF16) for i in range(2)]
                vh = [sb(st, "vh%d" % i, [128, S // 128, 128], BF16) for i in range(2)]
                qn = [sb(st, "qn%d" % i, [128, TT], BF16) for i in range(2)]
                qr = [sb(st, "qr%d" % i, [128, TT], BF16) for i in range(2)]
                accD = [sb(st, "accD%d" % i, [128, TT], F32) for i in range(2)]
                sg = [sb(st, "sg%d" % i, [128, TT], BF16) for i in range(2)]
                pt = [sb(st, "pt%d" % i, [128, TT], BF16) for i in range(4)]
                rden = [sb(st, "rden%d" % i, [128, TT], F32) for i in range(2)]
                ot = [sb(st, "ot%d" % i, [128, TT], F32) for i in range(2)]
                og = [sb(st, "og%d" % i, [128, TT], BF16) for i in range(2)]
                wrm = sb(st, "wrm", [128, TT], BF16)
                sch.op("pool", lambda e: e.memset(wrm[:], 1.0), [], [wrm])

                def warm(n):
                    if WARM_N == 0:
                        return
                    n = WARM_N
                    sch.op("pe", lambda e: [e.matmul(ps[7][:], ones_b[:], wrm[:], start=(i_ == 0), stop=(i_ == n - 1)) for i_ in range(n)][-1],
                           [ones_b, wrm], [ps[7]])
                nkt = ntile * 4
                sch.op("pool", lambda e: e.memset(krope[64:128, :], 0.0), [], [krope])
                for q_ in qr:
                    sch.op("pool", lambda e: e.memset(q_[64:128, :], 0.0), [], [q_])
                sch.dma("sp", krope[0:64, 0:ntile * TT], kTr[:, 0:ntile * TT], krope, reads=[dr["kr"]], writes=[krope])
                sch.dma("sp", fk_all[:, 0:nkt, :], Fkd[0:ntile * TT, :].rearrange("(t p) h -> p t h", p=128), fk_all,
                        reads=[dr["Fk"]], writes=[fk_all])
                it = 0
                pi_ = 0
                for hh in range(NH):
                    k_ = kn[hh % 2]
                    v_ = vh[hh % 2]
                    sch.dma("sp", k_[:, 0:ntile * TT], kTn[hh, :, 0:ntile * TT], k_, reads=[dr["kn"]], writes=[k_])
                    sch.dma("sp", v_[:, 0:nkt, :], Vd[0:ntile * TT, hh * 128:(hh + 1) * 128].rearrange("(t p) d -> p t d", p=128), v_,
                            reads=[dr["V"]], writes=[v_])
                    warm(24)
                    for j in range(ntile):
                        tsl = slice(j * TT, (j + 1) * TT)
                        q_n, q_r, s_g = qn[it % 2], qr[it % 2], sg[it % 2]
                        pO, pD = ps[3 + 2 * (it % 2)], ps[4 + 2 * (it % 2)]
                        sch.dma("sp", q_n[:], qTn[hh, :, tsl], q_n, reads=[dr["qn"]], writes=[q_n])
                        sch.dma("sp", q_r[0:64, :], qTr[hh, :, tsl], q_r, reads=[dr["qr"]], writes=[q_r])
                        sch.dma("sp", s_g[:], smgT[hh * 128:(hh + 1) * 128, tsl], s_g, reads=[dr["smg"]], writes=[s_g])
                        nkc = 4 * (j + 1)
                        if it % 2 == 0:
                            emit_casts(1)
                        aD = accD[it % 2]
                        if DEN_MODE != "pe":
                            sch.op("pool", lambda e: e.memset(aD[:], 0.0), [], [aD])
                        slots = {}

                        def geom(kc):
                            r = kc - 4 * j
                            return r, (128 * r if r > 0 else 0), slice(kc * 128, (kc + 1) * 128)

                        def s_stage(kc):
                            nonlocal pi_
                            r, c0, ksl = geom(kc)
                            pS = ps[pi_ % 3]
                            p_ = pt[pi_ % 4]
                            pi_ += 1
                            slots[kc] = (pS, p_)

                            def smm(e):
                                e.matmul(pS[:, c0:TT], k_[:, ksl], q_n[:, c0:TT], start=True, stop=False)
                                return e.matmul(pS[:, c0:TT], krope[:, ksl], q_r[:, c0:TT], start=False, stop=True)
                            sch.op("pe", smm, [k_, krope, q_n, q_r], [pS])

                        def e_stage(kc):
                            r, c0, ksl = geom(kc)
                            pS, p_ = slots[kc]
                            sch.op("act", lambda e: e.activation(out=p_[:, c0:TT], in_=pS[:, c0:TT], func=AF.Exp,
                                                                 scale=fk_all[:, kc, hh:hh + 1]), [pS, fk_all], [p_])
                            if r >= 0:
                                sch.op("pool", lambda e: e.memset(p_[64:128, c0:c0 + 64], 0.0), [p_], [p_])
                            if DEN_MODE == "dve" or (DEN_MODE == "split" and kc % 2 == 1):
                                sch.op("dve", lambda e: e.tensor_tensor(aD[:, c0:TT], aD[:, c0:TT], p_[:, c0:TT], ALU.add), [aD, p_], [aD])

                        def pv_stage(kc):
                            r, c0, ksl = geom(kc)
                            pS, p_ = slots[kc]
                            def pvmm(e):
                                ins = e.matmul(pO[:, c0:TT], v_[:, kc, :], p_[:, c0:TT], start=(kc == 0), stop=(kc == nkc - 1))
                                if DEN_MODE == "pe" or (DEN_MODE == "split" and kc % 2 == 0):
                                    ins = e.matmul(pD[:, c0:TT], ones_b[:], p_[:, c0:TT], start=(kc == 0),
                                                   stop=(DEN_MODE == "pe" and kc == nkc - 1))
                                return ins
                            sch.op("pe", pvmm, [v_, p_, ones_b], [pO, pD])

                        s_stage(0)
                        s_stage(1)
                        for kc in range(nkc):
                            if kc + 2 < nkc:
                                s_stage(kc + 2)
                            e_stage(kc)
                            pv_stage(kc)
                        if DEN_MODE != "pe":
                            sch.op("pe", lambda e: e.matmul(pD[:], ones_f[:], aD[:], start=(DEN_MODE == "dve"), stop=True),
                                   [ones_f, aD], [pD])
                        rd, o_, g_ = rden[it % 2], ot[it % 2], og[it % 2]
                        sch.op("dve", lambda e: e.reciprocal(rd[:], pD[:]), [pD], [rd])
                        sch.op("dve", lambda e: e.tensor_tensor(o_[:], pO[:], rd[:], ALU.mult), [pO, rd], [o_])
                        sch.op("pool", lambda e: e.tensor_tensor(g_[:], o_[:], s_g[:], ALU.mult), [o_, s_g], [g_])
                        sch.dma("pool", attT[hh * 128:(hh + 1) * 128, tsl], g_[:], g_, reads=[g_], writes=[dr["att"]])
                        it += 1
                sch.barrier()

        def phase_C(l, xsrc, xsrc_buf, xdst, xdst_buf):
            emit_casts(len(cast_jobs))
            ntile = NTILE if ntile_limit is None else ntile_limit
            with ExitStack() as st:
                cb = [sb(st, "cb%d" % i, [128, NCH, TT], BF16) for i in range(2)]
                ab = [sb(st, "ab%d" % i, [128, NCH, TT], BF16) for i in range(2)]
                yb = sb(st, "yb", [128, NCH, TT], BF16)
                wblk = [sb(st, "wc%d" % i, [128, NCH, 512], BF16) for i in range(3)]
                wr = {"i": 0}
                gcs = [sb(st, "gcs%d" % i, [128, TT], BF16) for i in range(3)]
                gms = [sb(st, "gms%d" % i, [128, TT], BF16) for i in range(3)]
                xs = [sb(st, "xs%d" % i, [128, TT], F32) for i in range(3)]
                t1 = [sb(st, "t1%d" % i, [128, TT], F32) for i in range(2)]
                t2 = [sb(st, "t2%d" % i, [128, TT], F32) for i in range(2)]
                xo = [sb(st, "xo%d" % i, [128, TT], F32) for i in range(3)]

                def load_w(wb, key, c0):
                    b = wblk[wr["i"] % len(wblk)]
                    wr["i"] += 1
                    sch.dma("sp", b[:], wview(wb, c0, c0 + 512), b, reads=[dw[key]], writes=[b])
                    return b

                for j in range(ntile):
                    tsl = slice(j * TT, (j + 1) * TT)
                    c_, a_ = cb[j % 2], ab[j % 2]
                    sch.dma("sp", c_[:], cbrT.rearrange("(c p) t -> p c t", p=128)[:, :, tsl], c_, reads=[dr["cbr"]], writes=[c_])
                    sch.dma("sp", a_[:], attT.rearrange("(c p) t -> p c t", p=128)[:, :, tsl], a_, reads=[dr["att"]], writes=[a_])
                    for mb in range(4):
                        wpc = load_w(wb_pc[l], ("pc", l), mb * 512)
                        wpm = load_w(wb_pm[l], ("pm", l), mb * 512)
                        for mi in range(4):
                            m = mb * 4 + mi
                            gc_, gm_ = gcs[m % 3], gms[m % 3]
                            sch.dma("sp", gc_[:], tgcT[m * 128:(m + 1) * 128, tsl], gc_, reads=[dr["tgc"]], writes=[gc_])
                            sch.dma("sp", gm_[:], tgmT[m * 128:(m + 1) * 128, tsl], gm_, reads=[dr["tgm"]], writes=[gm_])
                            pYc, pYm = psg(), psg()
                            mm16(pYc, wpc, mi * 128, 128, c_, lambda k: c_[:, k, :])
                            mm16(pYm, wpm, mi * 128, 128, a_, lambda k: a_[:, k, :])
                            u1, u2 = t1[m % 2], t2[m % 2]
                            sch.op("dve", lambda e: e.scalar_tensor_tensor(out=u1[:], in0=gc_[:], scalar=1.0, in1=pYc[:],
                                                                           op0=ALU.add, op1=ALU.mult), [gc_, pYc], [u1])
                            sch.op("dve", lambda e: e.scalar_tensor_tensor(out=u2[:], in0=gm_[:], scalar=1.0, in1=pYm[:],
                                                                           op0=ALU.add, op1=ALU.mult), [gm_, pYm], [u2])
                            sch.op("pool", lambda e: e.tensor_tensor(yb[:, m, :], u1[:], u2[:], ALU.add), [u1, u2], [yb])
                    for mb in range(4):
                        wo = load_w(wb_o[l], ("o", l), mb * 512)
                        for mi in range(4):
                            m = mb * 4 + mi
                            x_ = xs[m % 3]
                            sch.dma("sp", x_[:], xsrc[m * 128:(m + 1) * 128, tsl], x_, reads=[xsrc_buf], writes=[x_])
                            pO = psg()
                            mm16(pO, wo, mi * 128, 128, yb, lambda k: yb[:, k, :])
                            o_ = xo[m % 3]
                            sch.op("dve", lambda e: e.scalar_tensor_tensor(out=o_[:], in0=pO[:], scalar=0.5, in1=x_[:],
                                                                           op0=ALU.mult, op1=ALU.add), [pO, x_], [o_])
                            sch.dma("pool", xdst[m * 128:(m + 1) * 128, tsl], o_[:], o_, reads=[o_], writes=[xdst_buf])
                sch.barrier()

        for l in range(depth):
            xsrc, xsb = (xT_in, d_x0) if l == 0 else (x1T, dr["x1"])
            xdst, xdb = (y_out, dr["y"]) if l == depth - 1 else (x1T, dr["x1"])
            if "A" in phases:
                phase_A(l, xsrc, xsb)
            if "B" in phases:
                phase_B(l)
            if "C" in phases:
                phase_C(l, xsrc, xsb, xdst, xdb)
        sch.barrier()
    return nc


def _prep_common(inputs, depth=DEPTH):
    f = np.float32

    def pc(v):
        return np.ascontiguousarray(np.asarray(v, f).reshape(depth, NCH, 128).transpose(2, 0, 1))
    pvec = np.stack([pc(inputs["norm_g"][:depth]), pc(inputs["conv_b"][:depth]), pc(inputs["conv_ln_g"][:depth]),
                     pc(inputs["conv_ln_b"][:depth]), pc(inputs["conv_ln_b"][:depth])], axis=1)
    cw = np.asarray(inputs["conv_w"][:depth], f)
    convwT = np.ascontiguousarray(cw.reshape(depth, CW, NCH, 128).transpose(3, 0, 2, 1))

    def p4(v):
        return np.asarray(v, f).reshape(depth, 4, 128).transpose(2, 0, 1)
    lorag = np.ascontiguousarray(np.stack([p4(inputs["q_a_g"][:depth]), p4(inputs["kv_a_g"][:depth])], axis=1))
    qg = np.asarray(inputs["q_norm_g"][:depth], f)
    kg = np.asarray(inputs["k_norm_g"][:depth], f)
    hn_n = np.ascontiguousarray(np.stack([qg[:, :128].T, kg[:, :128].T], axis=1))
    hn_r = np.ascontiguousarray(np.stack([qg[:, 128:].T, kg[:, 128:].T], axis=1))
    consts = np.zeros((128, 193), f)
    consts[:, :128] = np.eye(128, dtype=f)
    Rm = np.zeros((64, 64), f)
    for p in range(32):
        Rm[p + 32, p] = -1.0
        Rm[p, p + 32] = 1.0
    consts[:64, 128:192] = Rm
    inv = (np.float32(10000.0) ** (-(np.arange(0, 64, 2, dtype=np.float32)) / np.float32(64))).astype(f)
    consts[:64, 192] = np.concatenate([inv, inv])
    common = {
        "w_in": np.ascontiguousarray(np.asarray(inputs["w_in"][:depth], f)),
        "w_uq": np.ascontiguousarray(np.asarray(inputs["w_uq"][:depth], f)),
        "w_ukv": np.ascontiguousarray(np.asarray(inputs["w_ukv"][:depth], f)),
        "w_proj_conv": np.ascontiguousarray(np.asarray(inputs["w_proj_conv"][:depth], f)),
        "w_proj_mla": np.ascontiguousarray(np.asarray(inputs["w_proj_mla"][:depth], f)),
        "w_out": np.ascontiguousarray(np.asarray(inputs["w_out"][:depth], f)),
        "pvec": np.ascontiguousarray(pvec), "convwT": convwT, "lorag": lorag, "hnorm_n": hn_n, "hnorm_r": hn_r,
        "consts": consts,
    }
    return common


def kernel(x, positions, norm_g, w_in, conv_w, conv_b, conv_ln_g, conv_ln_b, q_a_g, w_uq, kv_a_g, w_ukv,
           q_norm_g, k_norm_g, w_proj_conv, w_proj_mla, w_out):
    inputs = dict(norm_g=norm_g, w_in=w_in, conv_w=conv_w, conv_b=conv_b, conv_ln_g=conv_ln_g, conv_ln_b=conv_ln_b,
                  q_a_g=q_a_g, w_uq=w_uq, kv_a_g=kv_a_g, w_ukv=w_ukv, q_norm_g=q_norm_g, k_norm_g=k_norm_g,
                  w_proj_conv=w_proj_conv, w_proj_mla=w_proj_mla, w_out=w_out)
    common = _prep_common(inputs)
    x = np.asarray(x, np.float32)
    positions = np.asarray(positions, np.int32)
    n = x.shape[0]
    in_maps = []
    for b in range(n):
        m = dict(common)
        m["xT"] = np.ascontiguousarray(x[b].T)
        m["pos"] = np.ascontiguousarray(positions[b].reshape(1, S))
        in_maps.append(m)
    nc = build_program()
    res = run_bass_kernel_spmd(nc, in_maps, core_ids=list(range(n)))
    out = np.stack([np.ascontiguousarray(np.asarray(r["yT"]).T) for r in res.results], axis=0)
    return out.astype(np.float32)
```

```python
from contextlib import ExitStack
import numpy as np
import concourse.bass as bass
import concourse.mybir as mybir
from concourse.bass_utils import run_bass_kernel_spmd

F32 = mybir.dt.float32
BF16 = mybir.dt.bfloat16
I32 = mybir.dt.int32
AF = mybir.ActivationFunctionType
ALU = mybir.AluOpType

D = 2048
S = 4096
DEPTH = 2
NCH = 16
TT = 512
NTILE = S // TT
CW = 31
NH = 16
IN_DIM = 13376
EPS = 1e-6
OFF_A, OFF_GLU, OFF_CG, OFF_QD, OFF_KVD, OFF_KPE, OFF_MG, OFF_GC, OFF_GM = (
    0, 2048, 4096, 6144, 6656, 7168, 7232, 9280, 11328)
PI = float(np.pi)
A_PARTS = {"A1", "A2", "A3"}
A3_STOP = 9
WARM_N = 16
DEN_MODE = "pe"


class Buf:
    __slots__ = ("t", "w", "r", "ds", "acc", "excl")

    def __init__(self, t, acc=False, excl=False):
        self.excl = excl
        self.t = t
        self.w = {}
        self.r = {}
        self.ds = None
        self.acc = acc

    def __getitem__(self, idx):
        return self.t[idx]


class DSem:
    __slots__ = ("h", "tot", "key")

    def __init__(self, h, key):
        self.h = h
        self.tot = 0
        self.key = key


class Sched:
    def __init__(self, nc, stack, n_dsem=56):
        self.nc = nc
        self.eng = {"pe": nc.tensor, "act": nc.scalar, "dve": nc.vector, "pool": nc.gpsimd, "sp": nc.sync}
        self.sem = {k: stack.enter_context(nc.semaphore("cs_" + k)) for k in ("pe", "act", "dve", "pool")}
        self.cnt = {k: 0 for k in self.sem}
        self.seen = {k: {} for k in self.eng}
        self.all_dsems = [DSem(stack.enter_context(nc.semaphore("ds%d" % i)), "ds%d" % i) for i in range(n_dsem)]
        self.free_dsems = {"pool": list(self.all_dsems[:26]), "sp": list(self.all_dsems[26:])}
        self.ds_kind = {}
        for k_, lst in self.free_dsems.items():
            for d_ in lst:
                self.ds_kind[d_.key] = k_
        self.phase_bufs = []

    def _gather(self, reads, writes):
        deps = {}

        def add(d):
            for k, sv in d.items():
                if k not in deps or deps[k][1] < sv[1]:
                    deps[k] = sv
        for b in reads:
            add(b.w)
            if b.excl:
                add(b.r)
        for b in writes:
            if not b.acc:
                add(b.w)
            add(b.r)
        return deps

    def _wait(self, e, deps):
        seen = self.seen[e]
        for k, (sem, val) in deps.items():
            if seen.get(k, 0) < val:
                self.eng[e].wait_ge(sem, val)
                seen[k] = val

    def _record(self, key, d, reads, writes):
        for b in writes:
            if b.acc:
                b.w[key] = d
            else:
                b.w = {key: d}
                b.r = {}
        for b in reads:
            b.r[key] = d

    def op(self, e, fn, reads=(), writes=(), inc=True):
        deps = self._gather(reads, writes)
        if e == "pe":
            deps.pop("pe", None)
        self._wait(e, deps)
        ins = fn(self.eng[e])
        if inc:
            self.cnt[e] += 1
            ins.then_inc(self.sem[e], 1)
            self._record(e, (self.sem[e], self.cnt[e]), reads, writes)
        return ins

    def dsem_for(self, owner, q):
        if owner.ds is None:
            owner.ds = self.free_dsems[q].pop()
            self.phase_bufs.append(owner)
        assert self.ds_kind[owner.ds.key] == q, "buffer used with both DMA queue kinds"
        return owner.ds

    def dma(self, q, out, in_, owner, reads=(), writes=()):
        deps = self._gather(reads, writes)
        self._wait(q, deps)
        ds = self.dsem_for(owner, q)
        ins = self.eng[q].dma_start(out=out, in_=in_)
        ds.tot += 16
        ins.then_inc(ds.h, 16)
        self._record(ds.key, (ds.h, ds.tot), reads, writes)
        return ins

    def barrier(self, engines=("pe", "act", "dve", "pool", "sp"), skip_keep=False):
        deps = {k: (self.sem[k], self.cnt[k]) for k in self.sem if self.cnt[k] > 0}
        keep_keys = {b.ds.key for b in self.phase_bufs if getattr(b, "_keep", False) and b.ds is not None}
        for ds in self.all_dsems:
            if ds.tot > 0 and not (skip_keep and ds.key in keep_keys):
                deps[ds.key] = (ds.h, ds.tot)
        for e in engines:
            d = dict(deps)
            if e in d:
                pass
            self._wait(e, d)
        for b in self.phase_bufs:
            if not getattr(b, "_keep", False):
                self.free_dsems[self.ds_kind[b.ds.key]].append(b.ds)
                b.ds = None
        self.phase_bufs = [b for b in self.phase_bufs if b.ds is not None]


def build_program(depth=DEPTH, debug=False, phases="ABC", ntile_limit=None):
    nc = bass.Bass("TRN2", target_bir_lowering=False)
    dbg_kind = "ExternalOutput" if debug else "Internal"

    def din(name, shape, dt=F32):
        return nc.dram_tensor(name, list(shape), dt, kind="ExternalInput").ap()

    def dscr(name, shape, dt, dbg=True):
        return nc.dram_tensor(name, list(shape), dt, kind=(dbg_kind if dbg else "Internal")).ap()

    xT_in = din("xT", [D, S])
    pos_in = din("pos", [1, S], I32)
    w_in = din("w_in", [depth, D, IN_DIM])
    w_uq = din("w_uq", [depth, 512, 3072])
    w_ukv = din("w_ukv", [depth, 512, 4096])
    w_pc = din("w_proj_conv", [depth, D, D])
    w_pm = din("w_proj_mla", [depth, D, D])
    w_o = din("w_out", [depth, D, D])
    pv_in = din("pvec", [128, 5, depth, NCH])
    cw_in = din("convwT", [128, depth, NCH, CW])
    lg_in = din("lorag", [128, 2, depth, 4])
    hn_in = din("hnorm_n", [128, 2, depth])
    hr_in = din("hnorm_r", [64, 2, depth])
    cst_in = din("consts", [128, 128 + 64 + 1])
    y_out = nc.dram_tensor("yT", [D, S], F32, kind="ExternalOutput").ap()

    wb_in = [dscr("wb_in%d" % l, [D, IN_DIM], BF16, False) for l in range(depth)]
    wb_uq = [dscr("wb_uq%d" % l, [512, 3072], BF16, False) for l in range(depth)]
    wb_uk = [dscr("wb_uk%d" % l, [512, NH, 128], BF16, False) for l in range(depth)]
    wb_uv = [dscr("wb_uv%d" % l, [512, NH, 128], BF16, False) for l in range(depth)]
    wb_pc = [dscr("wb_pc%d" % l, [D, D], BF16, False) for l in range(depth)]
    wb_pm = [dscr("wb_pm%d" % l, [D, D], BF16, False) for l in range(depth)]
    wb_o = [dscr("wb_o%d" % l, [D, D], BF16, False) for l in range(depth)]
    cosT = dscr("cosT", [64, S], F32)
    sinT = dscr("sinT", [64, S], F32)
    cbrT = dscr("cbrT", [D, S], BF16)
    smgT = dscr("smgT", [D, S], BF16)
    tgcT = dscr("tgcT", [D, S], BF16)
    tgmT = dscr("tgmT", [D, S], BF16)
    qTn = dscr("qTn", [NH, 128, S], BF16)
    qTr = dscr("qTr", [NH, 64, S], BF16)
    kTn = dscr("kTn", [NH, 128, S], BF16)
    kTr = dscr("kTr", [64, S], BF16)
    Vd = dscr("Vd", [S, D], BF16)
    Fkd = dscr("Fkd", [S, NH], F32)
    attT = dscr("attT", [D, S], BF16)
    x1T = dscr("x1T", [D, S], F32)

    with ExitStack() as stack:
        sch = Sched(nc, stack)

        uid = {"i": 0}

        def uname(name):
            uid["i"] += 1
            return "s%d_%s" % (uid["i"], name)

        def sb(st, name, shape, dt, keep=False):
            b = Buf(st.enter_context(nc.sbuf_tensor(uname(name), list(shape), dt)))
            if keep:
                b._keep = True
            return b

        class KBuf(Buf):
            __slots__ = ("_keep",)

        def sbk(st, name, shape, dt):
            b = KBuf(st.enter_context(nc.sbuf_tensor(uname(name), list(shape), dt)))
            b._keep = False
            return b

        def dbuf():
            b = KBuf(None, acc=True)
            b._keep = False
            return b

        dr = {n: dbuf() for n in ("cos", "cbr", "smg", "tgc", "tgm", "qn", "qr", "kn", "kr", "V", "Fk", "att", "x1", "y")}
        dw = {}
        for l in range(depth):
            for n in ("ina", "inb", "uq", "uk", "uv", "pc", "pm", "o"):
                b = dbuf()
                b._keep = True
                dw[(n, l)] = b
        d_x0 = dbuf()

        ps = [Buf(stack.enter_context(nc.psum_tensor("ps%d" % i, [128, 512], F32)), excl=True) for i in range(8)]
        rot = {"i": 0}

        def psg(n=5):
            b = ps[rot["i"] % n]
            rot["i"] += 1
            return b

        ident_f = sbk(stack, "ident_f", [128, 193], F32)
        ident_b = sbk(stack, "ident_b", [128, 128], BF16)
        Rm_b = sbk(stack, "Rm_b", [64, 64], BF16)
        ones_b = sbk(stack, "ones_b", [128, 128], BF16)
        ones_d = sbk(stack, "ones_d", [128, 128], BF16)
        ones_q = sbk(stack, "ones_q", [128, 128], BF16)
        ones_f = sbk(stack, "ones_f", [128, 128], F32)
        negpi = sbk(stack, "negpi", [128, 1], F32)
        epsb = sbk(stack, "epsb", [128, 2], F32)
        pvec = sbk(stack, "pvec", [128, 5, depth, NCH], F32)
        lnb_h = sbk(stack, "lnb_h", [128, depth, NCH], F32)
        cwT = sbk(stack, "cwT", [128, depth, NCH, CW], F32)
        lorag = sbk(stack, "lorag", [128, 2, depth, 4], F32)
        hn_n = sbk(stack, "hn_n", [128, 2, depth], F32)
        hn_r = sbk(stack, "hn_r", [64, 2, depth], F32)

        for dst, src in ((ident_f, cst_in), (pvec, pv_in), (cwT, cw_in), (lorag, lg_in), (hn_n, hn_in), (hn_r, hr_in)):
            sch.dma("sp", dst[:], src[:], dst, writes=[dst])
        sch.op("dve", lambda e: e.tensor_copy(ident_b[:], ident_f[:, 0:128]), [ident_f], [ident_b])
        sch.op("dve", lambda e: e.tensor_copy(Rm_b[:], ident_f[0:64, 128:192]), [ident_f], [Rm_b])
        sch.op("pool", lambda e: e.memset(ones_b[:], 1.0), [], [ones_b])
        sch.op("pool", lambda e: e.memset(ones_d[:], 1.0 / 2048.0), [], [ones_d])
        sch.op("pool", lambda e: e.memset(ones_q[:], 1.0 / 512.0), [], [ones_q])
        sch.op("pool", lambda e: e.memset(ones_f[:], 1.0), [], [ones_f])
        sch.op("pool", lambda e: e.memset(negpi[:], -PI), [], [negpi])
        sch.op("pool", lambda e: e.memset(epsb[:, 0:1], EPS), [], [epsb])
        sch.op("pool", lambda e: e.memset(epsb[:, 1:2], 192.0 * EPS), [], [epsb])

        def rsq(out_ap, out_buf, in_ap, in_buf, tmp_ap, tmp_buf, power=-0.5, eps_col=0, rows=128):
            sch.op("act", lambda e: e.activation(out=tmp_ap, in_=in_ap, func=AF.Ln, bias=epsb[0:rows, eps_col:eps_col + 1], scale=1.0),
                   [in_buf, epsb], [tmp_buf])
            sch.op("act", lambda e: e.activation(out=out_ap, in_=tmp_ap, func=AF.Exp, scale=power), [tmp_buf], [out_buf])


        cast_jobs = []

        def cast(dst, src, key, rows, cols, rb=512, cb=2048, cbeg=0):
            for r0 in range(0, rows, rb):
                for c0 in range(cbeg, cols, cb):
                    c1 = min(cols, c0 + cb)
                    cast_jobs.append(lambda r0=r0, c0=c0, c1=c1: sch.dma(
                        "pool", dst[r0:r0 + rb, c0:c1], src[r0:r0 + rb, c0:c1], dw[key], writes=[dw[key]]))

        def cast_kv(l):
            kv4 = w_ukv[l].rearrange("k (h two d) -> k h two d", two=2, d=128)
            for r0 in range(0, 512, 128):
                cast_jobs.append(lambda r0=r0: sch.dma("pool", wb_uk[l][r0:r0 + 128], kv4[r0:r0 + 128, :, 0, :], dw[("uk", l)],
                                                       writes=[dw[("uk", l)]]))
                cast_jobs.append(lambda r0=r0: sch.dma("pool", wb_uv[l][r0:r0 + 128], kv4[r0:r0 + 128, :, 1, :], dw[("uv", l)],
                                                       writes=[dw[("uv", l)]]))

        n_upfront = None
        for l in range(depth):
            cast(wb_in[l], w_in[l], ("ina", l), D, OFF_QD)
            cast(wb_in[l], w_in[l], ("inb", l), D, IN_DIM, cbeg=OFF_QD)
            cast(wb_uq[l], w_uq[l], ("uq", l), 512, 3072)
            cast_kv(l)
            if n_upfront is None:
                n_upfront = len(cast_jobs)
            cast(wb_pc[l], w_pc[l], ("pc", l), D, D)
            cast(wb_pm[l], w_pm[l], ("pm", l), D, D)
            cast(wb_o[l], w_o[l], ("o", l), D, D)

        def emit_casts(n):
            for _ in range(min(n, len(cast_jobs))):
                cast_jobs.pop(0)()

        with ExitStack() as st:
            posi = sb(st, "posi", [64, S], I32)
            ang = sb(st, "ang", [64, S], F32)
            tmp = sb(st, "tmpT", [64, S], F32)
            tab = sb(st, "tabT", [64, S], F32)
            pos_b = pos_in.to_broadcast([64, S])
            sch.dma("sp", posi[:], pos_b, posi, writes=[posi])
            sch.op("dve", lambda e: e.tensor_copy(ang[:], posi[:]), [posi], [ang])
            sch.op("dve", lambda e: e.tensor_scalar(ang[:], ang[:], ident_f[0:64, 192:193], None, ALU.mult), [ang, ident_f], [ang])
            for shift, dstT in ((0.0, sinT), (0.5 * PI, cosT)):
                sch.op("dve", lambda e: e.tensor_scalar(tmp[:], ang[:], shift, 1.0 / (2.0 * PI), ALU.add, ALU.mult), [ang], [tmp])
                sch.op("dve", lambda e: e.tensor_copy(posi[:], tmp[:]), [tmp], [posi])
                sch.op("dve", lambda e: e.tensor_copy(tmp[:], posi[:]), [posi], [tmp])
                sch.op("dve", lambda e: e.scalar_tensor_tensor(out=tmp[:], in0=tmp[:], scalar=-2.0 * PI, in1=ang[:],
                                                               op0=ALU.mult, op1=ALU.add), [tmp, ang], [tmp])
                sch.op("dve", lambda e: e.tensor_scalar(tmp[:], tmp[:], shift, None, ALU.add), [tmp], [tmp])
                sch.op("dve", lambda e: e.tensor_scalar(tmp[:], tmp[:], -PI, PI, ALU.max, ALU.min), [tmp], [tmp])
                sch.op("act", lambda e: e.activation(out=tab[:], in_=tmp[:], func=AF.Sin), [tmp], [tab])
                sch.dma("pool", dstT[:], tab[:], tab, reads=[tab], writes=[dr["cos"]])
            sch.barrier(skip_keep=True)
        emit_casts(n_upfront)

        def wview(wb, c0, c1):
            return wb.rearrange("(k p) n -> p k n", p=128)[:, :, c0:c1]

        def mm16(pbuf, wblk, coff, m, rhs_buf, rhs_fn, extra_reads=(), nk=NCH, mrows=128):
            def fn(e):
                ins = None
                for k in range(nk):
                    ins = e.matmul(pbuf[0:m, :] if m < 128 else pbuf[:], wblk[:, k, coff:coff + m], rhs_fn(k),
                                   start=(k == 0), stop=(k == nk - 1))
                return ins
            return sch.op("pe", fn, [wblk, rhs_buf] + list(extra_reads), [pbuf])

        def phase_A(l, xsrc, xsrc_buf):
            if l > 0:
                emit_casts(len(cast_jobs))
            g_norm = lambda c: pvec[:, 0, l, c:c + 1]
            g_cb = lambda c: pvec[:, 1, l, c:c + 1]
            g_lg = lambda c: pvec[:, 2, l, c:c + 1]
            g_lb = lambda c: pvec[:, 3, l, c:c + 1]
            ntile = NTILE if ntile_limit is None else ntile_limit
            with ExitStack() as st:
                xt = [sb(st, "xt%d" % i, [128, TT], F32) for i in range(3)]
                xr = {"i": 0}
                hT = [sb(st, "hT%d" % i, [128, NCH, TT], BF16) for i in range(2)]
                sq = [sb(st, "sq%d" % i, [128, TT], BF16) for i in range(2)]
                rstd = sb(st, "rstd", [128, TT], F32)
                lnt = sb(st, "lnt", [128, TT], F32)
                wblk = [sb(st, "wblk%d" % i, [128, NCH, 256], BF16) for i in range(4)]
                wr = {"i": 0}
                vst = sb(st, "vst", [128, NCH, TT], BF16)
                uall = sb(st, "uall", [128, NCH, TT + 32], BF16)
                vstb = [Buf(vst.t) for _ in range(NCH)]
                uallb = [Buf(uall.t) for _ in range(NCH)]
                dg = [sb(st, "dg%d" % i, [128, CW, 128], BF16) for i in range(2)]
                tg = [sb(st, "tg%d" % i, [128, TT], F32) for i in range(2)]
                v2 = [sb(st, "v2%d" % i, [128, TT], BF16) for i in range(2)]
                lnA = sb(st, "lnA", [128, TT], F32)
                lnB = sb(st, "lnB", [128, TT], F32)
                lnm = sb(st, "lnm", [128, TT], F32)
                sgt = [sb(st, "sgt%d" % i, [128, TT], F32) for i in range(1)]
                zt = [sb(st, "zt%d" % i, [128, TT], F32) for i in range(1)]
                szt = [sb(st, "szt%d" % i, [128, TT], F32) for i in range(1)]
                stg = [sb(st, "stg%d" % i, [128, TT], BF16) for i in range(4)]
                sr = {"i": 0}
                qd = sb(st, "qd", [128, 4, TT], BF16)
                qsq = sb(st, "qsq", [128, 4, TT], BF16)
                kvd = sb(st, "kvd", [128, 4, TT], BF16)
                kvsq = sb(st, "kvsq", [128, 4, TT], BF16)
                Rq = sb(st, "Rq", [128, TT], F32)
                Rq2 = sb(st, "Rq2", [128, TT], F32)
                Sk = sb(st, "Sk", [64, TT], F32)
                rkv_tm = sb(st, "rkv_tm", [128, 4], F32)
                r2kv_tm = sb(st, "r2kv_tm", [128, 4], F32)
                sspe_tm = sb(st, "sspe_tm", [128, 4], F32)
                fk_tm = sb(st, "fk_tm", [128, 4, NH], F32)
                cs_t = sb(st, "cs_t", [64, 2, TT], F32)
                kpsq = sb(st, "kpsq", [64, TT], BF16)
                rf = [sb(st, "rf%d" % i, [64, TT], F32) for i in range(2)]
                rb_ = [sb(st, "rb%d" % i, [64, TT], BF16) for i in range(2)]
                r1 = [sb(st, "r1%d" % i, [64, TT], F32) for i in range(2)]
                r2 = [sb(st, "r2%d" % i, [64, TT], F32) for i in range(2)]
                Ft = [sb(st, "Ft%d" % i, [128, TT], F32) for i in range(2)]
                qsn = [sb(st, "qsn%d" % i, [128, TT], BF16) for i in range(2)]
                qsr = [sb(st, "qsr%d" % i, [64, TT], BF16) for i in range(2)]
                ksq = [sb(st, "ksq%d" % i, [128, TT], BF16) for i in range(2)]
                stg64 = [sb(st, "stg64_%d" % i, [64, TT], BF16) for i in range(2)]
                s64 = {"i": 0}

                def next_stage():
                    b = stg[sr["i"] % len(stg)]
                    sr["i"] += 1
                    return b

                def next_stage64():
                    b = stg64[s64["i"] % len(stg64)]
                    s64["i"] += 1
                    return b

                def load_w(wb, key, c0, ncols, nk=NCH):
                    if key is None:
                        key = ("ina", l) if c0 < OFF_QD else ("inb", l)
                    b = wblk[wr["i"] % len(wblk)]
                    wr["i"] += 1
                    sch.dma("sp", b[:, 0:nk, 0:ncols], wview(wb, c0, c0 + ncols), b, reads=[dw[key]], writes=[b])
                    return b

                sch.op("pool", lambda e: e.memset(uall[:], 0.0), [], uallb)
                pS1, pS2, pT = ps[5], ps[6], ps[7]

                def a0_steps(jn):
                    tsn = slice(jn * TT, (jn + 1) * TT)
                    hn = hT[jn % 2]
                    pss = ps[7]

                    def load_x(c):
                        x_t = xt[xr["i"] % len(xt)]
                        xr["i"] += 1
                        sch.dma("sp", x_t[:], xsrc[c * 128:(c + 1) * 128, tsn], x_t, reads=[xsrc_buf], writes=[x_t])
                        return x_t
                    steps = []

                    def st1(c):
                        if c < NCH:
                            x_t = load_x(c)
                            s_ = sq[c % 2]
                            sch.op("act", lambda e: e.activation(out=s_[:], in_=x_t[:], func=AF.Square), [x_t], [s_])
                        if c >= 1:
                            p_ = sq[(c - 1) % 2]
                            sch.op("pe", lambda e: e.matmul(pss[:], ones_d[:], p_[:], start=(c - 1 == 0), stop=(c - 1 == NCH - 1)),
                                   [ones_d, p_], [pss])

                    def st2():
                        rsq(rstd[:], rstd, pss[:], pss, lnt[:], lnt)

                    def st3(c):
                        x_t = load_x(c)
                        sch.op("dve", lambda e: e.scalar_tensor_tensor(out=hn[:, c, :], in0=x_t[:], scalar=g_norm(c),
                                                                       in1=rstd[:], op0=ALU.mult, op1=ALU.mult),
                               [x_t, rstd, pvec], [hn])
                    for c in range(NCH + 1):
                        steps.append(lambda c=c: st1(c))
                    steps.append(st2)
                    for c in range(NCH):
                        steps.append(lambda c=c: st3(c))
                    return steps

                for f_ in a0_steps(0):
                    f_()

                for j in range(ntile):
                    tsl = slice(j * TT, (j + 1) * TT)
                    h = hT[j % 2]
                    emit_casts(2)
                    a1w = {}

                    def a1_s1(m):
                        mb, mi = m // 2, m % 2
                        if mi == 0:
                            a1w["wa"] = load_w(wb_in[l], None, OFF_A + mb * 256, 256)
                            a1w["wg"] = load_w(wb_in[l], None, OFF_GLU + mb * 256, 256)
                        wa, wg = a1w["wa"], a1w["wg"]
                        pA, pG = ps[(m % 2) * 2], ps[(m % 2) * 2 + 1]
                        mm16(pA, wa, mi * 128, 128, h, lambda k: h[:, k, :])
                        mm16(pG, wg, mi * 128, 128, h, lambda k: h[:, k, :])
                        t_ = tg[m % 2]
                        sch.op("act", lambda e: e.activation(out=t_[:], in_=pG[:], func=AF.Tanh, scale=0.5), [pG], [t_])
                        if j > 0:
                            sch.op("pool", lambda e: e.tensor_copy(uall[:, m, 0:30], uall[:, m, TT:TT + 30]), [uallb[m]], [uallb[m]])
                        sch.op("dve", lambda e: e.scalar_tensor_tensor(out=uall[:, m, 30:30 + TT], in0=t_[:], scalar=1.0,
                                                                       in1=pA[:], op0=ALU.add, op1=ALU.mult),
                               [t_, pA], [uallb[m]])
                        d_ = dg[m % 2]
                        in0 = ident_b[:, None, :].to_broadcast([128, CW, 128])
                        in1 = cwT[:, l, m, :, None].to_broadcast([128, CW, 128])
                        sch.op("pool", lambda e: e.tensor_tensor(d_[:], in0, in1, ALU.mult), [ident_b, cwT], [d_])

                    def a1_s2(m):
                        d_ = dg[m % 2]
                        pC = ps[4] if m % 2 == 0 else ps[7]

                        def convmm(e):
                            ins = None
                            for k in range(CW):
                                ins = e.matmul(pC[:], d_[:, k, :], uall[:, m, k:k + TT], start=(k == 0), stop=(k == CW - 1))
                            return ins
                        sch.op("pe", convmm, [d_, uallb[m]], [pC])
                        sch.op("act", lambda e: e.activation(out=vst[:, m, :], in_=pC[:], func=AF.Identity,
                                                             bias=g_cb(m), scale=0.5), [pC, pvec], [vstb[m]])
                        q_ = v2[m % 2]
                        sch.op("act", lambda e: e.activation(out=q_[:], in_=vst[:, m, :], func=AF.Square), [vstb[m]], [q_])

                    def a1_s3(m):
                        q_ = v2[m % 2]
                        sch.op("pe", lambda e: e.matmul(pS1[:], ones_d[:], vst[:, m, :], start=(m == 0), stop=(m == NCH - 1)),
                               [ones_d, vstb[m]], [pS1])
                        sch.op("pe", lambda e: e.matmul(pS2[:], ones_d[:], q_[:], start=(m == 0), stop=(m == NCH - 1)),
                               [ones_d, q_], [pS2])

                    if "A1" in A_PARTS:
                        for i in range(NCH + 2):
                            if i < NCH:
                                a1_s1(i)
                            if 0 <= i - 1 < NCH:
                                a1_s2(i - 1)
                            if 0 <= i - 2 < NCH:
                                a1_s3(i - 2)
                    if "A1" not in A_PARTS:
                        sch.op("pe", lambda e: e.matmul(pS1[:], ones_d[:], ones_d[:, 0:1].to_broadcast([128, 512]) if False else sq[0][:], start=True, stop=True), [ones_d, sq[0]], [pS1])
                        sch.op("pe", lambda e: e.matmul(pS2[:], ones_d[:], sq[0][:], start=True, stop=True), [ones_d, sq[0]], [pS2])
                    sch.op("act", lambda e: e.activation(out=lnm[:], in_=pS1[:], func=AF.Identity), [pS1], [lnm])
                    sch.op("dve", lambda e: e.tensor_tensor(lnB[:], lnm[:], lnm[:], ALU.mult), [lnm], [lnB])
                    sch.op("dve", lambda e: e.tensor_tensor(lnA[:], pS2[:], lnB[:], ALU.subtract), [pS2, lnB], [lnA])
                    sch.op("dve", lambda e: e.tensor_scalar(lnA[:], lnA[:], 0.0, None, ALU.max), [lnA], [lnA])
                    rsq(lnA[:], lnA, lnA[:], lnA, lnt[:], lnt)
                    sch.op("dve", lambda e: e.scalar_tensor_tensor(out=lnB[:], in0=lnm[:], scalar=-1.0, in1=lnA[:],
                                                                   op0=ALU.mult, op1=ALU.mult), [lnm, lnA], [lnB])
                    for mb in (range(8) if "A1" in A_PARTS else ()):
                        wc = load_w(wb_in[l], None, OFF_CG + mb * 256, 256)
                        for mi in range(2):
                            m = mb * 2 + mi
                            pA = psg()
                            mm16(pA, wc, mi * 128, 128, h, lambda k: h[:, k, :])
                            s_ = sgt[0]
                            sch.op("act", lambda e: e.activation(out=s_[:], in_=pA[:], func=AF.Silu), [pA], [s_])
                            z_ = zt[0]
                            sch.op("dve", lambda e: e.tensor_tensor(z_[:], vst[:, m, :], lnA[:], ALU.mult), [vstb[m], lnA], [z_])
                            sch.op("dve", lambda e: e.tensor_tensor(z_[:], z_[:], lnB[:], ALU.add), [z_, lnB], [z_])
                            y_ = szt[0]
                            sch.op("act", lambda e: e.activation(out=y_[:], in_=z_[:], func=AF.Silu, bias=g_lb(m), scale=g_lg(m)),
                                   [z_, pvec], [y_])
                            o_ = next_stage()
                            sch.op("dve", lambda e: e.tensor_tensor(o_[:], y_[:], s_[:], ALU.mult), [y_, s_], [o_])
                            sch.dma("pool", cbrT[m * 128:(m + 1) * 128, tsl], o_[:], o_, reads=[o_], writes=[dr["cbr"]])

                    gate_tasks = []
                    if "A2" in A_PARTS:
                        for off, func, scale, dst, key in ((OFF_MG, AF.Silu, 1.0, smgT, "smg"), (OFF_GC, AF.Tanh, 0.5, tgcT, "tgc"),
                                                           (OFF_GM, AF.Tanh, 0.5, tgmT, "tgm")):
                            for m in range(NCH):
                                gate_tasks.append((off, func, scale, dst, key, m))
                    gw = {}
                    gcnt = {"i": 0}

                    def run_gate(fixed_banks=True):
                        if not gate_tasks:
                            return
                        off, func, scale, dst, key, m = gate_tasks.pop(0)
                        mb, mi = m // 2, m % 2
                        if mi == 0:
                            gw["w"] = load_w(wb_in[l], None, off + mb * 256, 256)
                        wc = gw["w"]
                        if fixed_banks:
                            pA = ps[5 + gcnt["i"] % 2]
                            gcnt["i"] += 1
                        else:
                            pA = psg()
                        mm16(pA, wc, mi * 128, 128, h, lambda k: h[:, k, :])
                        o_ = next_stage()
                        sch.op("act", lambda e: e.activation(out=o_[:], in_=pA[:], func=func, scale=scale), [pA], [o_])
                        sch.dma("pool", dst[m * 128:(m + 1) * 128, tsl], o_[:], o_, reads=[o_], writes=[dr[key]])

                    def finish_tile():
                        nxt = a0_steps(j + 1) if j + 1 < ntile else []
                        while gate_tasks or nxt:
                            run_gate(fixed_banks=False)
                            for _ in range(3):
                                if nxt:
                                    nxt.pop(0)()

                    if "A3" not in A_PARTS:
                        finish_tile()
                        continue
                    sch.dma("sp", cs_t[:, 0, :], cosT[:, tsl], cs_t, reads=[dr["cos"]], writes=[cs_t])
                    sch.dma("sp", cs_t[:, 1, :], sinT[:, tsl], cs_t, reads=[dr["cos"]], writes=[cs_t])
                    for (off, dstd, dsq, gi) in ((OFF_QD, qd, qsq, 0), (OFF_KVD, kvd, kvsq, 1)):
                        for c in range(4):
                            if c % 2 == 0:
                                wc = load_w(wb_in[l], None, off + c * 128, 256)
                            pA = psg()
                            mm16(pA, wc, (c % 2) * 128, 128, h, lambda k: h[:, k, :])
                            sch.op("act", lambda e: e.activation(out=dsq[:, c, :], in_=pA[:], func=AF.Square), [pA], [dsq])
                            sch.op("dve", lambda e: e.tensor_scalar(dstd[:, c, :], pA[:], lorag[:, gi, l, c:c + 1], None, ALU.mult),
                                   [pA, lorag], [dstd])
                    pq = psg()
                    sch.op("pe", lambda e: [e.matmul(pq[:], ones_q[:], qsq[:, c, :], start=(c == 0), stop=(c == 3)) for c in range(4)][-1],
                           [ones_q, qsq], [pq])
                    rsq(Rq[:], Rq, pq[:], pq, lnt[:], lnt)
                    sch.op("dve", lambda e: e.scalar_tensor_tensor(out=Rq2[:], in0=Rq[:], scalar=1.0 / 192.0, in1=Rq[:],
                                                                   op0=ALU.mult, op1=ALU.mult), [Rq], [Rq2])
                    pk = psg()
                    sch.op("pe", lambda e: [e.matmul(pk[:], ones_q[:], kvsq[:, c, :], start=(c == 0), stop=(c == 3)) for c in range(4)][-1],
                           [ones_q, kvsq], [pk])
                    rsq(Sk[:], Sk, pk[0:64, :], pk, lnt[0:64, :], lnt, power=0.5, rows=64)
                    def kvstat(e):
                        ins = None
                        for tb in range(4):
                            for c in range(4):
                                ins = e.matmul(pT[:, tb:tb + 1], kvsq[:, c, tb * 128:(tb + 1) * 128], ones_q[:, 0:1],
                                               start=(c == 0), stop=(c == 3))
                        return ins
                    sch.op("pe", kvstat, [kvsq, ones_q], [pT])
                    rsq(rkv_tm[:], rkv_tm, pT[:, 0:4], pT, lnt[:, 0:4], lnt)
                    sch.op("dve", lambda e: e.tensor_tensor(r2kv_tm[:], rkv_tm[:], rkv_tm[:], ALU.mult), [rkv_tm], [r2kv_tm])

                    if A3_STOP <= 1:
                        finish_tile()
                        continue
                    wc = load_w(wb_in[l], None, OFF_KPE, 64)
                    pK = psg()
                    mm16(pK, wc, 0, 64, h, lambda k: h[:, k, :])
                    sch.op("act", lambda e: e.activation(out=kpsq[:], in_=pK[0:64, :], func=AF.Square), [pK], [kpsq])

                    def pestat(e):
                        ins = None
                        for tb in range(4):
                            ins = e.matmul(pT[:, 4 + tb:5 + tb], kpsq[:, tb * 128:(tb + 1) * 128], ones_b[0:64, 0:1],
                                           start=True, stop=True)
                        return ins
                    sch.op("pe", pestat, [kpsq, ones_b], [pT])
                    sch.op("dve", lambda e: e.tensor_copy(sspe_tm[:], pT[:, 4:8]), [pT], [sspe_tm])

                    def rope(src_f, src_b, out_ap, out_buf, extra_mul=None, extra_buf=None, ri=0, pR=None):
                        pR = psg() if pR is None else pR
                        sch.op("pe", lambda e: e.matmul(pR[0:64, :], Rm_b[:], src_b[:], start=True, stop=True), [Rm_b, src_b], [pR])
                        a_, b_ = r1[ri], r2[ri]
                        sch.op("pool", lambda e: e.tensor_tensor(a_[:], src_f[:], cs_t[:, 0, :], ALU.mult), [src_f, cs_t], [a_])
                        sch.op("dve", lambda e: e.tensor_tensor(b_[:], pR[0:64, :], cs_t[:, 1, :], ALU.mult), [pR, cs_t], [b_])
                        if extra_mul is None:
                            sch.op("pool", lambda e: e.tensor_tensor(out_ap, a_[:], b_[:], ALU.add), [a_, b_], [out_buf])
                        else:
                            sch.op("pool", lambda e: e.tensor_tensor(a_[:], a_[:], b_[:], ALU.add), [a_, b_], [a_])
                            sch.op("pool", lambda e: e.tensor_tensor(out_ap, a_[:], extra_mul, ALU.mult), [a_, extra_buf], [out_buf])

                    sch.op("dve", lambda e: e.tensor_scalar(rf[0][:], pK[0:64, :], hn_r[:, 1, l:l + 1], None, ALU.mult), [pK, hn_r], [rf[0]])
                    sch.op("pool", lambda e: e.tensor_copy(rb_[0][:], rf[0][:]), [rf[0]], [rb_[0]])
                    o64 = next_stage64()
                    rope(rf[0], rb_[0], o64[:], o64, extra_mul=Sk[:], extra_buf=Sk, ri=0)
                    sch.dma("pool", kTr[:, tsl], o64[:], o64, reads=[o64], writes=[dr["kr"]])

                    if A3_STOP <= 2:
                        finish_tile()
                        continue
                    qw = {}

                    def q_s1(hh):
                        hg, hi = hh // 4, hh % 4
                        if hi == 0:
                            b = wblk[wr["i"] % len(wblk)]
                            wr["i"] += 1
                            wq3 = b[:].rearrange("p k n -> p (k n)")[:, 0:4 * 768].rearrange("p (k n) -> p k n", k=4)
                            sch.dma("sp", wq3, wb_uq[l].rearrange("(k p) n -> p k n", p=128)[:, :, hg * 768:(hg + 1) * 768], b,
                                    reads=[dw[("uq", l)]], writes=[b])
                            qw["b"], qw["w"] = b, wq3
                        b, wq3 = qw["b"], qw["w"]
                        ri = hh % 2
                        pQn, pQr = ps[ri * 2], ps[ri * 2 + 1]
                        sch.op("pe", lambda e: [e.matmul(pQn[:], wq3[:, c, hi * 192:hi * 192 + 128], qd[:, c, :],
                                                         start=(c == 0), stop=(c == 3)) for c in range(4)][-1], [b, qd], [pQn])
                        sch.op("pe", lambda e: [e.matmul(pQr[0:64, :], wq3[:, c, hi * 192 + 128:hi * 192 + 192], qd[:, c, :],
                                                         start=(c == 0), stop=(c == 3)) for c in range(4)][-1], [b, qd], [pQr])
                        sn, sr_ = qsn[ri], qsr[ri]
                        sch.op("act", lambda e: e.activation(out=sn[:], in_=pQn[:], func=AF.Square), [pQn], [sn])
                        sch.op("act", lambda e: e.activation(out=sr_[:], in_=pQr[0:64, :], func=AF.Square), [pQr], [sr_])

                    def q_s2(hh):
                        ri = hh % 2
                        pQn, pQr = ps[ri * 2], ps[ri * 2 + 1]
                        sn, sr_ = qsn[ri], qsr[ri]
                        pSS = ps[4]

                        def ssmm(e):
                            e.matmul(pSS[:], ones_b[:], sn[:], start=True, stop=False)
                            return e.matmul(pSS[:], ones_b[0:64, :], sr_[:], start=False, stop=True)
                        sch.op("pe", ssmm, [ones_b, sn, sr_], [pSS])
                        F_ = Ft[ri]
                        sch.op("dve", lambda e: e.tensor_tensor(F_[:], pSS[:], Rq2[:], ALU.mult), [pSS, Rq2], [F_])
                        rsq(F_[:], F_, F_[:], F_, lnt[:], lnt)
                        sch.op("dve", lambda e: e.tensor_tensor(F_[:], F_[:], Rq[:], ALU.mult), [F_, Rq], [F_])
                        o_ = next_stage()
                        sch.op("dve", lambda e: e.scalar_tensor_tensor(out=o_[:], in0=pQn[:], scalar=hn_n[:, 0, l:l + 1], in1=F_[:],
                                                                       op0=ALU.mult, op1=ALU.mult), [pQn, hn_n, F_], [o_])
                        sch.dma("pool", qTn[hh, :, tsl], o_[:], o_, reads=[o_], writes=[dr["qn"]])
                        sch.op("dve", lambda e: e.scalar_tensor_tensor(out=rf[ri][:], in0=pQr[0:64, :], scalar=hn_r[:, 0, l:l + 1],
                                                                       in1=F_[0:64, :], op0=ALU.mult, op1=ALU.mult),
                               [pQr, hn_r, F_], [rf[ri]])
                        sch.op("act", lambda e: e.activation(out=rb_[ri][:], in_=rf[ri][:], func=AF.Identity), [rf[ri]], [rb_[ri]])

                    def q_s3(hh):
                        ri = hh % 2
                        o64 = next_stage64()
                        rope(rf[ri], rb_[ri], o64[:], o64, ri=ri, pR=ps[7])
                        sch.dma("pool", qTr[hh, :, tsl], o64[:], o64, reads=[o64], writes=[dr["qr"]])

                    for i in range(NH + 2):
                        if i < NH:
                            q_s1(i)
                        if 0 <= i - 1 < NH:
                            q_s2(i - 1)
                        if 0 <= i - 2 < NH:
                            q_s3(i - 2)
                        run_gate()

                    if A3_STOP <= 3:
                        finish_tile()
                        continue
                    pF = ps[7]
                    kw = {}

                    def k_s1(hh):
                        if hh % 4 == 0:
                            bk = wblk[wr["i"] % len(wblk)]
                            wr["i"] += 1
                            wk3 = bk[:].rearrange("p k n -> p (k n)")[:, 0:4 * 512].rearrange("p (k n) -> p k n", k=4)
                            sch.dma("sp", wk3, wb_uk[l].rearrange("(k p) h d -> p k (h d)", p=128)[:, :, (hh // 4) * 512:(hh // 4 + 1) * 512],
                                    bk, reads=[dw[("uk", l)]], writes=[bk])
                            kw["b"], kw["w"] = bk, wk3
                        bk, wk3 = kw["b"], kw["w"]
                        pKn = psg()
                        sch.op("pe", lambda e: [e.matmul(pKn[:], wk3[:, c, (hh % 4) * 128:(hh % 4 + 1) * 128], kvd[:, c, :],
                                                         start=(c == 0), stop=(c == 3)) for c in range(4)][-1], [bk, kvd], [pKn])
                        s_ = ksq[hh % 2]
                        sch.op("act", lambda e: e.activation(out=s_[:], in_=pKn[:], func=AF.Square), [pKn], [s_])
                        o_ = next_stage()
                        sch.op("dve", lambda e: e.tensor_scalar(o_[:], pKn[:], hn_n[:, 1, l:l + 1], None, ALU.mult), [pKn, hn_n], [o_])
                        sch.dma("pool", kTn[hh, :, tsl], o_[:], o_, reads=[o_], writes=[dr["kn"]])

                    def k_s2(hh):
                        s_ = ksq[hh % 2]

                        def kstat(e):
                            ins = None
                            for tb in range(4):
                                ins = e.matmul(pF[:, 8 + tb * NH + hh:9 + tb * NH + hh], s_[:, tb * 128:(tb + 1) * 128], ones_b[:, 0:1],
                                               start=True, stop=True)
                            return ins
                        sch.op("pe", kstat, [s_, ones_b], [pF])

                    for i in range(NH + 1):
                        if i < NH:
                            k_s1(i)
                        if 0 <= i - 1 < NH:
                            k_s2(i - 1)
                        if i < NH:
                            run_gate()
                    ssn = pF[:, 8:8 + 4 * NH].rearrange("p (t h) -> p t h", t=4)
                    sch.op("dve", lambda e: e.tensor_tensor(fk_tm[:], ssn, r2kv_tm[:, :, None].to_broadcast([128, 4, NH]), ALU.mult),
                           [pF, r2kv_tm], [fk_tm])
                    sch.op("dve", lambda e: e.tensor_tensor(fk_tm[:], fk_tm[:], sspe_tm[:, :, None].to_broadcast([128, 4, NH]), ALU.add),
                           [fk_tm, sspe_tm], [fk_tm])
                    rsq(fk_tm[:], fk_tm, fk_tm[:], fk_tm, lnt[:, 0:4 * NH].rearrange("p (t h) -> p t h", t=4), lnt, eps_col=1)
                    sch.op("dve", lambda e: e.tensor_tensor(fk_tm[:], fk_tm[:], rkv_tm[:, :, None].to_broadcast([128, 4, NH]), ALU.mult),
                           [fk_tm, rkv_tm], [fk_tm])
                    sch.dma("pool", Fkd[tsl, :].rearrange("(t p) h -> p t h", p=128), fk_tm[:], fk_tm, reads=[fk_tm], writes=[dr["Fk"]])

                    if A3_STOP <= 4:
                        finish_tile()
                        continue
                    for hg in range(4):
                        bv = wblk[wr["i"] % len(wblk)]
                        wr["i"] += 1
                        wv3 = bv[:].rearrange("p k n -> p (k n)")[:, 0:4 * 512].rearrange("p (k n) -> p k n", k=4)
                        sch.dma("sp", wv3, wb_uv[l].rearrange("(k p) h d -> p k (h d)", p=128)[:, :, hg * 512:(hg + 1) * 512], bv,
                                reads=[dw[("uv", l)]], writes=[bv])
                        for tb in range(4):
                            pV = psg()
                            sch.op("pe", lambda e: [e.matmul(pV[:], kvd[:, c, tb * 128:(tb + 1) * 128], wv3[:, c, :],
                                                             start=(c == 0), stop=(c == 3)) for c in range(4)][-1], [bv, kvd], [pV])
                            o_ = next_stage()
                            sch.op("dve", lambda e: e.tensor_scalar(o_[:], pV[:], rkv_tm[:, tb:tb + 1], None, ALU.mult),
                                   [pV, rkv_tm], [o_])
                            sch.dma("pool", Vd[j * TT + tb * 128:j * TT + (tb + 1) * 128, hg * 512:(hg + 1) * 512], o_[:], o_,
                                    reads=[o_], writes=[dr["V"]])
                    finish_tile()
                sch.barrier()

        def phase_B(l):
            ntile = NTILE if ntile_limit is None else ntile_limit
            with ExitStack() as st:
                krope = sb(st, "krope", [128, S], BF16)
                fk_all = sb(st, "fk_all", [128, S // 128, NH], F32)
                kn = [sb(st, "kn%d" % i, [128, S], BF16) for i in range(2)]
                vh = [sb(st, "vh%d" % i, [128, S // 128, 128], BF16) for i in range(2)]
                qn = [sb(st, "qn%d" % i, [128, TT], BF16) for i in range(2)]
                qr = [sb(st, "qr%d" % i, [128, TT], BF16) for i in range(2)]
                accD = [sb(st, "accD%d" % i, [128, TT], F32) for i in range(2)]
                sg = [sb(st, "sg%d" % i, [128, TT], BF16) for i in range(2)]
                pt = [sb(st, "pt%d" % i, [128, TT], BF16) for i in range(4)]
                rden = [sb(st, "rden%d" % i, [128, TT], F32) for i in range(2)]
                ot = [sb(st, "ot%d" % i, [128, TT], F32) for i in range(2)]
                og = [sb(st, "og%d" % i, [128, TT], BF16) for i in range(2)]
                wrm = sb(st, "wrm", [128, TT], BF16)
                sch.op("pool", lambda e: e.memset(wrm[:], 1.0), [], [wrm])

                def warm(n):
                    if WARM_N == 0:
                        return
                    n = WARM_N
                    sch.op("pe", lambda e: [e.matmul(ps[7][:], ones_b[:], wrm[:], start=(i_ == 0), stop=(i_ == n - 1)) for i_ in range(n)][-1],
                           [ones_b, wrm], [ps[7]])
                nkt = ntile * 4
                sch.op("pool", lambda e: e.memset(krope[64:128, :], 0.0), [], [krope])
                for q_ in qr:
                    sch.op("pool", lambda e: e.memset(q_[64:128, :], 0.0), [], [q_])
                sch.dma("sp", krope[0:64, 0:ntile * TT], kTr[:, 0:ntile * TT], krope, reads=[dr["kr"]], writes=[krope])
                sch.dma("sp", fk_all[:, 0:nkt, :], Fkd[0:ntile * TT, :].rearrange("(t p) h -> p t h", p=128), fk_all,
                        reads=[dr["Fk"]], writes=[fk_all])
                it = 0
                pi_ = 0
                for hh in range(NH):
                    k_ = kn[hh % 2]
                    v_ = vh[hh % 2]
                    sch.dma("sp", k_[:, 0:ntile * TT], kTn[hh, :, 0:ntile * TT], k_, reads=[dr["kn"]], writes=[k_])
                    sch.dma("sp", v_[:, 0:nkt, :], Vd[0:ntile * TT, hh * 128:(hh + 1) * 128].rearrange("(t p) d -> p t d", p=128), v_,
                            reads=[dr["V"]], writes=[v_])
                    warm(24)
                    for j in range(ntile):
                        tsl = slice(j * TT, (j + 1) * TT)
                        q_n, q_r, s_g = qn[it % 2], qr[it % 2], sg[it % 2]
                        pO, pD = ps[3 + 2 * (it % 2)], ps[4 + 2 * (it % 2)]
                        sch.dma("sp", q_n[:], qTn[hh, :, tsl], q_n, reads=[dr["qn"]], writes=[q_n])
                        sch.dma("sp", q_r[0:64, :], qTr[hh, :, tsl], q_r, reads=[dr["qr"]], writes=[q_r])
                        sch.dma("sp", s_g[:], smgT[hh * 128:(hh + 1) * 128, tsl], s_g, reads=[dr["smg"]], writes=[s_g])
                        nkc = 4 * (j + 1)
                        if it % 2 == 0:
                            emit_casts(1)
                        aD = accD[it % 2]
                        if DEN_MODE != "pe":
                            sch.op("pool", lambda e: e.memset(aD[:], 0.0), [], [aD])
                        slots = {}

                        def geom(kc):
                            r = kc - 4 * j
                            return r, (128 * r if r > 0 else 0), slice(kc * 128, (kc + 1) * 128)

                        def s_stage(kc):
                            nonlocal pi_
                            r, c0, ksl = geom(kc)
                            pS = ps[pi_ % 3]
                            p_ = pt[pi_ % 4]
                            pi_ += 1
                            slots[kc] = (pS, p_)

                            def smm(e):
                                e.matmul(pS[:, c0:TT], k_[:, ksl], q_n[:, c0:TT], start=True, stop=False)
                                return e.matmul(pS[:, c0:TT], krope[:, ksl], q_r[:, c0:TT], start=False, stop=True)
                            sch.op("pe", smm, [k_, krope, q_n, q_r], [pS])

                        def e_stage(kc):
                            r, c0, ksl = geom(kc)
                            pS, p_ = slots[kc]
                            sch.op("act", lambda e: e.activation(out=p_[:, c0:TT], in_=pS[:, c0:TT], func=AF.Exp,
                                                                 scale=fk_all[:, kc, hh:hh + 1]), [pS, fk_all], [p_])
                            if r >= 0:
                                sch.op("pool", lambda e: e.memset(p_[64:128, c0:c0 + 64], 0.0), [p_], [p_])
                            if DEN_MODE == "dve" or (DEN_MODE == "split" and kc % 2 == 1):
                                sch.op("dve", lambda e: e.tensor_tensor(aD[:, c0:TT], aD[:, c0:TT], p_[:, c0:TT], ALU.add), [aD, p_], [aD])

                        def pv_stage(kc):
                            r, c0, ksl = geom(kc)
                            pS, p_ = slots[kc]
                            def pvmm(e):
                                ins = e.matmul(pO[:, c0:TT], v_[:, kc, :], p_[:, c0:TT], start=(kc == 0), stop=(kc == nkc - 1))
                                if DEN_MODE == "pe" or (DEN_MODE == "split" and kc % 2 == 0):
                                    ins = e.matmul(pD[:, c0:TT], ones_b[:], p_[:, c0:TT], start=(kc == 0),
                                                   stop=(DEN_MODE == "pe" and kc == nkc - 1))
                                return ins
                            sch.op("pe", pvmm, [v_, p_, ones_b], [pO, pD])

                        s_stage(0)
                        s_stage(1)
                        for kc in range(nkc):
                            if kc + 2 < nkc:
                                s_stage(kc + 2)
                            e_stage(kc)
                            pv_stage(kc)
                        if DEN_MODE != "pe":
                            sch.op("pe", lambda e: e.matmul(pD[:], ones_f[:], aD[:], start=(DEN_MODE == "dve"), stop=True),
                                   [ones_f, aD], [pD])
                        rd, o_, g_ = rden[it % 2], ot[it % 2], og[it % 2]
                        sch.op("dve", lambda e: e.reciprocal(rd[:], pD[:]), [pD], [rd])
                        sch.op("dve", lambda e: e.tensor_tensor(o_[:], pO[:], rd[:], ALU.mult), [pO, rd], [o_])
                        sch.op("pool", lambda e: e.tensor_tensor(g_[:], o_[:], s_g[:], ALU.mult), [o_, s_g], [g_])
                        sch.dma("pool", attT[hh * 128:(hh + 1) * 128, tsl], g_[:], g_, reads=[g_], writes=[dr["att"]])
                        it += 1
                sch.barrier()

        def phase_C(l, xsrc, xsrc_buf, xdst, xdst_buf):
            emit_casts(len(cast_jobs))
            ntile = NTILE if ntile_limit is None else ntile_limit
            with ExitStack() as st:
                cb = [sb(st, "cb%d" % i, [128, NCH, TT], BF16) for i in range(2)]
                ab = [sb(st, "ab%d" % i, [128, NCH, TT], BF16) for i in range(2)]
                yb = sb(st, "yb", [128, NCH, TT], BF16)
                wblk = [sb(st, "wc%d" % i, [128, NCH, 512], BF16) for i in range(4)]
                wr = {"i": 0}
                gcs = [sb(st, "gcs%d" % i, [128, TT], BF16) for i in range(3)]
                gms = [sb(st, "gms%d" % i, [128, TT], BF16) for i in range(3)]
                xs = [sb(st, "xs%d" % i, [128, TT], F32) for i in range(3)]
                t1 = [sb(st, "t1%d" % i, [128, TT], F32) for i in range(2)]
                t2 = [sb(st, "t2%d" % i, [128, TT], F32) for i in range(2)]
                xo = [sb(st, "xo%d" % i, [128, TT], F32) for i in range(3)]

                def load_w(wb, key, c0):
                    b = wblk[wr["i"] % len(wblk)]
                    wr["i"] += 1
                    sch.dma("sp", b[:], wview(wb, c0, c0 + 512), b, reads=[dw[key]], writes=[b])
                    return b

                for j in range(ntile):
                    tsl = slice(j * TT, (j + 1) * TT)
                    c_, a_ = cb[j % 2], ab[j % 2]
                    sch.dma("sp", c_[:], cbrT.rearrange("(c p) t -> p c t", p=128)[:, :, tsl], c_, reads=[dr["cbr"]], writes=[c_])
                    sch.dma("sp", a_[:], attT.rearrange("(c p) t -> p c t", p=128)[:, :, tsl], a_, reads=[dr["att"]], writes=[a_])
                    for mb in range(4):
                        wpc = load_w(wb_pc[l], ("pc", l), mb * 512)
                        wpm = load_w(wb_pm[l], ("pm", l), mb * 512)
                        for mi in range(4):
                            m = mb * 4 + mi
                            gc_, gm_ = gcs[m % 3], gms[m % 3]
                            sch.dma("sp", gc_[:], tgcT[m * 128:(m + 1) * 128, tsl], gc_, reads=[dr["tgc"]], writes=[gc_])
                            sch.dma("sp", gm_[:], tgmT[m * 128:(m + 1) * 128, tsl], gm_, reads=[dr["tgm"]], writes=[gm_])
                            pYc, pYm = psg(), psg()
                            mm16(pYc, wpc, mi * 128, 128, c_, lambda k: c_[:, k, :])
                            mm16(pYm, wpm, mi * 128, 128, a_, lambda k: a_[:, k, :])
                            u1, u2 = t1[m % 2], t2[m % 2]
                            sch.op("dve", lambda e: e.scalar_tensor_tensor(out=u1[:], in0=gc_[:], scalar=1.0, in1=pYc[:],
                                                                           op0=ALU.add, op1=ALU.mult), [gc_, pYc], [u1])
                            sch.op("dve", lambda e: e.scalar_tensor_tensor(out=u2[:], in0=gm_[:], scalar=1.0, in1=pYm[:],
                                                                           op0=ALU.add, op1=ALU.mult), [gm_, pYm], [u2])
                            sch.op("pool", lambda e: e.tensor_tensor(yb[:, m, :], u1[:], u2[:], ALU.add), [u1, u2], [yb])
                    for mb in range(4):
                        wo = load_w(wb_o[l], ("o", l), mb * 512)
                        for mi in range(4):
                            m = mb * 4 + mi
                            x_ = xs[m % 3]
                            sch.dma("sp", x_[:], xsrc[m * 128:(m + 1) * 128, tsl], x_, reads=[xsrc_buf], writes=[x_])
                            pO = psg()
                            mm16(pO, wo, mi * 128, 128, yb, lambda k: yb[:, k, :])
                            o_ = xo[m % 3]
                            sch.op("dve", lambda e: e.scalar_tensor_tensor(out=o_[:], in0=pO[:], scalar=0.5, in1=x_[:],
                                                                           op0=ALU.mult, op1=ALU.add), [pO, x_], [o_])
                            sch.dma("pool", xdst[m * 128:(m + 1) * 128, tsl], o_[:], o_, reads=[o_], writes=[xdst_buf])
                sch.barrier()

        for l in range(depth):
            xsrc, xsb = (xT_in, d_x0) if l == 0 else (x1T, dr["x1"])
            xdst, xdb = (y_out, dr["y"]) if l == depth - 1 else (x1T, dr["x1"])
            if "A" in phases:
                phase_A(l, xsrc, xsb)
            if "B" in phases:
                phase_B(l)
            if "C" in phases:
                phase_C(l, xsrc, xsb, xdst, xdb)
        sch.barrier()
    return nc


def _prep_common(inputs, depth=DEPTH):
    f = np.float32

    def pc(v):
        return np.ascontiguousarray(np.asarray(v, f).reshape(depth, NCH, 128).transpose(2, 0, 1))
    pvec = np.stack([pc(inputs["norm_g"][:depth]), pc(inputs["conv_b"][:depth]), pc(inputs["conv_ln_g"][:depth]),
                     pc(inputs["conv_ln_b"][:depth]), pc(inputs["conv_ln_b"][:depth])], axis=1)
    cw = np.asarray(inputs["conv_w"][:depth], f)
    convwT = np.ascontiguousarray(cw.reshape(depth, CW, NCH, 128).transpose(3, 0, 2, 1))

    def p4(v):
        return np.asarray(v, f).reshape(depth, 4, 128).transpose(2, 0, 1)
    lorag = np.ascontiguousarray(np.stack([p4(inputs["q_a_g"][:depth]), p4(inputs["kv_a_g"][:depth])], axis=1))
    qg = np.asarray(inputs["q_norm_g"][:depth], f)
    kg = np.asarray(inputs["k_norm_g"][:depth], f)
    hn_n = np.ascontiguousarray(np.stack([qg[:, :128].T, kg[:, :128].T], axis=1))
    hn_r = np.ascontiguousarray(np.stack([qg[:, 128:].T, kg[:, 128:].T], axis=1))
    consts = np.zeros((128, 193), f)
    consts[:, :128] = np.eye(128, dtype=f)
    Rm = np.zeros((64, 64), f)
    for p in range(32):
        Rm[p + 32, p] = -1.0
        Rm[p, p + 32] = 1.0
    consts[:64, 128:192] = Rm
    inv = (np.float32(10000.0) ** (-(np.arange(0, 64, 2, dtype=np.float32)) / np.float32(64))).astype(f)
    consts[:64, 192] = np.concatenate([inv, inv])
    common = {
        "w_in": np.ascontiguousarray(np.asarray(inputs["w_in"][:depth], f)),
        "w_uq": np.ascontiguousarray(np.asarray(inputs["w_uq"][:depth], f)),
        "w_ukv": np.ascontiguousarray(np.asarray(inputs["w_ukv"][:depth], f)),
        "w_proj_conv": np.ascontiguousarray(np.asarray(inputs["w_proj_conv"][:depth], f)),
        "w_proj_mla": np.ascontiguousarray(np.asarray(inputs["w_proj_mla"][:depth], f)),
        "w_out": np.ascontiguousarray(np.asarray(inputs["w_out"][:depth], f)),
        "pvec": np.ascontiguousarray(pvec), "convwT": convwT, "lorag": lorag, "hnorm_n": hn_n, "hnorm_r": hn_r,
        "consts": consts,
    }
    return common


def kernel(x, positions, norm_g, w_in, conv_w, conv_b, conv_ln_g, conv_ln_b, q_a_g, w_uq, kv_a_g, w_ukv,
           q_norm_g, k_norm_g, w_proj_conv, w_proj_mla, w_out):
    inputs = dict(norm_g=norm_g, w_in=w_in, conv_w=conv_w, conv_b=conv_b, conv_ln_g=conv_ln_g, conv_ln_b=conv_ln_b,
                  q_a_g=q_a_g, w_uq=w_uq, kv_a_g=kv_a_g, w_ukv=w_ukv, q_norm_g=q_norm_g, k_norm_g=k_norm_g,
                  w_proj_conv=w_proj_conv, w_proj_mla=w_proj_mla, w_out=w_out)
    common = _prep_common(inputs)
    x = np.asarray(x, np.float32)
    positions = np.asarray(positions, np.int32)
    n = x.shape[0]
    in_maps = []
    for b in range(n):
        m = dict(common)
        m["xT"] = np.ascontiguousarray(x[b].T)
        m["pos"] = np.ascontiguousarray(positions[b].reshape(1, S))
        in_maps.append(m)
    nc = build_program()
    res = run_bass_kernel_spmd(nc, in_maps, core_ids=list(range(n)))
    out = np.stack([np.ascontiguousarray(np.asarray(r["yT"]).T) for r in res.results], axis=0)
    return out.astype(np.float32)
```
